# Optimizing a Trainium2 kernel written in Bass

```python
import jax, jax.numpy as jnp
from jax import lax
import numpy as np

D_MODEL = 2048
BATCH = 32
SEQ = 256
DEPTH = 4
DEC_BATCH = 8
DEC_SEQ = 1024
PAST_LEN = 256

GRID_W = 64
N_MIXERS = 2
N_ATTN_LAYERS = (DEPTH + 1) // 2
N_REC_LAYERS = DEPTH // 2
N_HEADS = 16
Q_LORA = 512
KV_LORA = 512
NOPE_DIM = 128
ROPE_DIM = 64
V_DIM = 128
ROPE_BASE = 10000.0
Q_BLOCK = 128
LRU_WIDTH = D_MODEL
N_LRU_BLOCKS = 16
LRU_BLOCK = LRU_WIDTH // N_LRU_BLOCKS
LRU_CONV_W = 4
LRU_CONV_PAD_LEFT = 2
LRU_C = 8.0
D_FF = 5632
FFN_CONV_W = 3
FFN_CONV_PAD_LEFT = 1
NORM_EPS = 1e-6

kernel_name = "hybrid_mla_rglru_convffn_diffusion_step"


def rmsnorm(x, g):
    xf = x.astype(jnp.float32)
    y = xf * lax.rsqrt(jnp.mean(xf * xf, axis=-1, keepdims=True) + NORM_EPS)
    return (y * g.astype(jnp.float32)).astype(x.dtype)


def adaln(cond, w, b):
    m = jax.nn.silu(cond) @ w + b
    return [t[:, None, :] for t in jnp.split(m, 6, axis=-1)]


def depthwise_conv(x, w, b, pad_left):
    k = w.shape[0]
    s = x.shape[1]
    xp = jnp.pad(x, ((0, 0), (pad_left, k - 1 - pad_left), (0, 0)))
    out = b
    for j in range(k):
        out = out + xp[:, j:j + s] * w[j]
    return out


def axial_rope_tables(n_tokens):
    rows = n_tokens // GRID_W
    row = jnp.repeat(jnp.arange(rows), GRID_W).astype(jnp.float32)
    col = jnp.tile(jnp.arange(GRID_W), rows).astype(jnp.float32)
    half = ROPE_DIM // 2
    inv = 1.0 / (ROPE_BASE ** (jnp.arange(0, half, 2, dtype=jnp.float32) / half))
    ang = jnp.concatenate([row[:, None] * inv, col[:, None] * inv], axis=-1)
    return jnp.cos(ang), jnp.sin(ang)


def apply_axial_rope(x, cos, sin):
    q = ROPE_DIM // 4
    xr1, xr2, xc1, xc2 = jnp.split(x.astype(jnp.float32), 4, axis=-1)
    cr, cc = cos[..., :q], cos[..., q:]
    sr, sc = sin[..., :q], sin[..., q:]
    y = jnp.concatenate([xr1 * cr - xr2 * sr, xr2 * cr + xr1 * sr,
                         xc1 * cc - xc2 * sc, xc2 * cc + xc1 * sc], axis=-1)
    return y.astype(x.dtype)


def mla_project(h, w_in, g_q, g_kv, w_uq):
    proj = h @ w_in
    c_q, c_kv, k_rope = jnp.split(proj, [Q_LORA, Q_LORA + KV_LORA], axis=-1)
    q = rmsnorm(c_q, g_q) @ w_uq
    q = q.reshape(q.shape[0], q.shape[1], N_HEADS, NOPE_DIM + ROPE_DIM)
    return q[..., :NOPE_DIM], q[..., NOPE_DIM:], rmsnorm(c_kv, g_kv), k_rope


def mla_expand(ckv, w_uk, w_uv):
    b, s = ckv.shape[:2]
    k_nope = (ckv @ w_uk).reshape(b, s, N_HEADS, NOPE_DIM)
    v = (ckv @ w_uv).reshape(b, s, N_HEADS, V_DIM)
    return k_nope, v


def mla_attention(q_nope, q_rope, k_nope, k_rope, v):
    b, sq = q_nope.shape[:2]
    nblk = sq // Q_BLOCK
    scale = (NOPE_DIM + ROPE_DIM) ** -0.5

    def blk(qs):
        qn, qr = qs
        s = (jnp.einsum('bqhd,bkhd->bhqk', qn, k_nope, preferred_element_type=jnp.float32)
             + jnp.einsum('bqhr,bkr->bhqk', qr, k_rope, preferred_element_type=jnp.float32))
        p = jax.nn.softmax(s * scale, axis=-1).astype(v.dtype)
        return jnp.einsum('bhqk,bkhd->bqhd', p, v)

    to_blocks = lambda t: jnp.moveaxis(t.reshape(b, nblk, Q_BLOCK, *t.shape[2:]), 1, 0)
    out = lax.map(blk, (to_blocks(q_nope), to_blocks(q_rope)))
    return jnp.moveaxis(out, 0, 1).reshape(b, sq, N_HEADS * V_DIM)


def block_diag(x, w, b):
    xb = x.reshape(x.shape[0], x.shape[1], N_LRU_BLOCKS, LRU_BLOCK)
    return jnp.einsum('bsni,nij->bsnj', xb, w).reshape(x.shape) + b


def rg_lru(x, lam, w_gx, b_gx, w_ga, b_ga, h0, reverse):
    gate_x = jax.nn.sigmoid(block_diag(x, w_gx, b_gx).astype(jnp.float32))
    gate_a = jax.nn.sigmoid(block_diag(x, w_ga, b_ga).astype(jnp.float32))
    log_a = -LRU_C * gate_a * jax.nn.softplus(-lam.astype(jnp.float32))
    a = jnp.exp(log_a)
    u = jnp.sqrt(-jnp.expm1(2.0 * log_a)) * (gate_x * x)

    def step(h, au):
        a_t, u_t = au
        h = a_t * h + u_t
        return h, h

    h_last, hs = lax.scan(step, h0.astype(jnp.float32),
                          (jnp.swapaxes(a, 0, 1), jnp.swapaxes(u, 0, 1)), reverse=reverse)
    return jnp.swapaxes(hs, 0, 1), h_last


def recurrent_mixer(h, w_in, w_conv, b_conv, w_gx, b_gx, w_ga, b_ga, lam, w_out, h0_f, h0_b):
    proj = h @ w_in
    y, xb = jnp.split(proj, 2, axis=-1)
    xb = depthwise_conv(xb, w_conv, b_conv, LRU_CONV_PAD_LEFT).astype(jnp.float32)
    hf, sf = rg_lru(xb, lam[0], w_gx[0], b_gx[0], w_ga[0], b_ga[0], h0_f, False)
    hb, sb = rg_lru(xb, lam[1], w_gx[1], b_gx[1], w_ga[1], b_ga[1], h0_b, True)
    mixed = ((hf + hb) * jax.nn.gelu(y.astype(jnp.float32))).astype(h.dtype)
    return mixed @ w_out, sf.astype(h.dtype), sb.astype(h.dtype)


def conv_ffn(h, w_up, w_conv, b_conv, w_down):
    u = depthwise_conv(h @ w_up, w_conv, b_conv, FFN_CONV_PAD_LEFT)
    g, v = jnp.split(u, 2, axis=-1)
    return (jax.nn.silu(g) * v) @ w_down


def setup_inputs(seed: int = 0) -> dict:
    key = jax.random.key(seed)
    ks = iter(jax.random.split(key, 40))
    D = D_MODEL

    def nrm(shape, s=1.0):
        return jax.random.normal(next(ks), shape, jnp.float32) * s

    def gain(shape):
        return 1.0 + nrm(shape, 0.02)

    inp = {}
    inp["x_prompt"] = nrm((BATCH, SEQ, D))
    inp["x_sample"] = nrm((DEC_BATCH, DEC_SEQ, D))
    inp["cache_ckv"] = nrm((DEC_BATCH, N_ATTN_LAYERS, PAST_LEN, KV_LORA))
    inp["cache_krope"] = nrm((DEC_BATCH, N_ATTN_LAYERS, PAST_LEN, ROPE_DIM))
    inp["state_lru"] = nrm((DEC_BATCH, N_REC_LAYERS, 2, LRU_WIDTH), 0.5)
    inp["c"] = nrm((DEC_BATCH, D))
    inp["c_ctx"] = nrm((D,))
    inp["g_mix"] = gain((DEPTH, D))
    inp["g_ffn"] = gain((DEPTH, D))
    inp["g_final"] = gain((D,))
    inp["w_ada"] = nrm((DEPTH, D, 6 * D), 0.5 * D ** -0.5)
    inp["b_ada"] = nrm((DEPTH, 6 * D), 0.02)
    inp["w_mla_in"] = nrm((N_ATTN_LAYERS, D, Q_LORA + KV_LORA + ROPE_DIM), D ** -0.5)
    inp["g_mla_q"] = gain((N_ATTN_LAYERS, Q_LORA))
    inp["g_mla_kv"] = gain((N_ATTN_LAYERS, KV_LORA))
    inp["w_mla_uq"] = nrm((N_ATTN_LAYERS, Q_LORA, N_HEADS * (NOPE_DIM + ROPE_DIM)), Q_LORA ** -0.5)
    inp["w_mla_uk"] = nrm((N_ATTN_LAYERS, KV_LORA, N_HEADS * NOPE_DIM), KV_LORA ** -0.5)
    inp["w_mla_uv"] = nrm((N_ATTN_LAYERS, KV_LORA, N_HEADS * V_DIM), KV_LORA ** -0.5)
    inp["w_mla_o"] = nrm((N_ATTN_LAYERS, N_HEADS * V_DIM, D), (N_HEADS * V_DIM) ** -0.5)
    inp["w_rec_in"] = nrm((N_REC_LAYERS, D, 2 * LRU_WIDTH), D ** -0.5)
    inp["w_rec_conv"] = nrm((N_REC_LAYERS, LRU_CONV_W, LRU_WIDTH), LRU_CONV_W ** -0.5)
    inp["b_rec_conv"] = nrm((N_REC_LAYERS, LRU_WIDTH), 0.01)
    inp["w_rec_gx"] = nrm((N_REC_LAYERS, 2, N_LRU_BLOCKS, LRU_BLOCK, LRU_BLOCK), LRU_BLOCK ** -0.5)
    inp["b_rec_gx"] = nrm((N_REC_LAYERS, 2, LRU_WIDTH), 0.01)
    inp["w_rec_ga"] = nrm((N_REC_LAYERS, 2, N_LRU_BLOCKS, LRU_BLOCK, LRU_BLOCK), LRU_BLOCK ** -0.5)
    inp["b_rec_ga"] = nrm((N_REC_LAYERS, 2, LRU_WIDTH), 0.01)
    a0 = jax.random.uniform(next(ks), (N_REC_LAYERS, 2, LRU_WIDTH), jnp.float32, 0.9, 0.999)
    inp["rec_lambda"] = jnp.log(a0) - jnp.log1p(-a0)
    inp["w_rec_out"] = nrm((N_REC_LAYERS, LRU_WIDTH, D), LRU_WIDTH ** -0.5)
    inp["w_ffn_up"] = nrm((DEPTH, D, 2 * D_FF), D ** -0.5)
    inp["w_ffn_conv"] = nrm((DEPTH, FFN_CONV_W, 2 * D_FF), FFN_CONV_W ** -0.5)
    inp["b_ffn_conv"] = nrm((DEPTH, 2 * D_FF), 0.01)
    inp["w_ffn_down"] = nrm((DEPTH, D_FF, D), D_FF ** -0.5)
    return inp


def reference(x_prompt, x_sample, cache_ckv, cache_krope, state_lru, c, c_ctx,
              g_mix, g_ffn, g_final, w_ada, b_ada,
              w_mla_in, g_mla_q, g_mla_kv, w_mla_uq, w_mla_uk, w_mla_uv, w_mla_o,
              w_rec_in, w_rec_conv, b_rec_conv, w_rec_gx, b_rec_gx, w_rec_ga, b_rec_ga,
              rec_lambda, w_rec_out,
              w_ffn_up, w_ffn_conv, b_ffn_conv, w_ffn_down):
    xp = x_prompt
    ckv_new, kr_new, lru_new = [], [], []
    for layer in range(DEPTH):
        sh1, sc1, g1, sh2, sc2, g2 = adaln(c_ctx[None, :], w_ada[layer], b_ada[layer])
        h = rmsnorm(xp, g_mix[layer]) * (1 + sc1) + sh1
        j = layer // N_MIXERS
        if layer % N_MIXERS == 0:
            qn, qr, ckv, kr = mla_project(h, w_mla_in[j], g_mla_q[j], g_mla_kv[j], w_mla_uq[j])
            kn, v = mla_expand(ckv, w_mla_uk[j], w_mla_uv[j])
            out = mla_attention(qn, qr, kn, kr, v) @ w_mla_o[j]
            ckv_new.append(ckv)
            kr_new.append(kr)
        else:
            h0 = jnp.zeros((xp.shape[0], LRU_WIDTH), jnp.float32)
            out, sf, sb = recurrent_mixer(h, w_rec_in[j], w_rec_conv[j], b_rec_conv[j],
                                          w_rec_gx[j], b_rec_gx[j], w_rec_ga[j], b_rec_ga[j],
                                          rec_lambda[j], w_rec_out[j], h0, h0)
            lru_new.append(jnp.stack([sf, sb], axis=1))
        xp = xp + g1 * out
        h = rmsnorm(xp, g_ffn[layer]) * (1 + sc2) + sh2
        xp = xp + g2 * conv_ffn(h, w_ffn_up[layer], w_ffn_conv[layer], b_ffn_conv[layer], w_ffn_down[layer])
    y_prompt = rmsnorm(xp, g_final)
    new_cache_ckv = jnp.stack(ckv_new, axis=1)
    new_cache_krope = jnp.stack(kr_new, axis=1)
    new_state_lru = jnp.stack(lru_new, axis=1)

    xs = x_sample
    cos, sin = axial_rope_tables(xs.shape[1])
    for layer in range(DEPTH):
        sh1, sc1, g1, sh2, sc2, g2 = adaln(c, w_ada[layer], b_ada[layer])
        h = rmsnorm(xs, g_mix[layer]) * (1 + sc1) + sh1
        j = layer // N_MIXERS
        if layer % N_MIXERS == 0:
            qn, qr, ckv_l, kr_l = mla_project(h, w_mla_in[j], g_mla_q[j], g_mla_kv[j], w_mla_uq[j])
            qr = apply_axial_rope(qr, cos[:, None, :], sin[:, None, :])
            kr_l = apply_axial_rope(kr_l, cos, sin)
            kn_c, v_c = mla_expand(cache_ckv[:, j], w_mla_uk[j], w_mla_uv[j])
            kn_l, v_l = mla_expand(ckv_l, w_mla_uk[j], w_mla_uv[j])
            kn = jnp.concatenate([kn_c, kn_l], axis=1)
            kr = jnp.concatenate([cache_krope[:, j], kr_l], axis=1)
            v = jnp.concatenate([v_c, v_l], axis=1)
            out = mla_attention(qn, qr, kn, kr, v) @ w_mla_o[j]
        else:
            out, _, _ = recurrent_mixer(h, w_rec_in[j], w_rec_conv[j], b_rec_conv[j],
                                        w_rec_gx[j], b_rec_gx[j], w_rec_ga[j], b_rec_ga[j],
                                        rec_lambda[j], w_rec_out[j],
                                        state_lru[:, j, 0], state_lru[:, j, 1])
        xs = xs + g1 * out
        h = rmsnorm(xs, g_ffn[layer]) * (1 + sc2) + sh2
        xs = xs + g2 * conv_ffn(h, w_ffn_up[layer], w_ffn_conv[layer], b_ffn_conv[layer], w_ffn_down[layer])
    y_sample = rmsnorm(xs, g_final)

    return (y_prompt, y_sample, new_cache_ckv, new_cache_krope, new_state_lru)
```

```python
import numpy as np
import concourse.bass as bass
import concourse.mybir as mybir

F32 = mybir.dt.float32
BF16 = mybir.dt.bfloat16
AF = mybir.ActivationFunctionType
ALU = mybir.AluOpType

ENG_NAMES = ["pe", "act", "dve", "pool", "sp"]


class Tile:
    __slots__ = ("w", "r")

    def __init__(self):
        self.w = None
        self.r = {}


class V:
    __slots__ = ("ap", "tiles")

    def __init__(self, ap, tiles):
        self.ap = ap
        self.tiles = tuple(tiles)

    def __getitem__(self, idx):
        return V(self.ap[idx], self.tiles)

    def bitcast(self, dt):
        return V(self.ap.bitcast(dt), self.tiles)

    def rearrange(self, pattern, **kw):
        return V(self.ap.rearrange(pattern, **kw), self.tiles)

    def sub(self, tiles):
        return V(self.ap, tiles)


def _ap(x):
    return x.ap if isinstance(x, V) else x


class Prog:
    def __init__(self, nc, n_dma_sems=32):
        self.nc = nc
        self.ops = {e: [] for e in ENG_NAMES}
        self.NS = n_dma_sems
        self.n_dma = 0
        self.n_dma_q = {}
        self.dma_uid = 0

    def _collect(self, reads, writes, me, rkey):
        deps = []
        for v in reads:
            for t in v.tiles:
                if t.w is not None:
                    deps.append(t.w)
        for v in writes:
            for t in v.tiles:
                if t.w is not None:
                    deps.append(t.w)
                deps.extend(t.r.values())
        for v in reads:
            for t in v.tiles:
                t.r[rkey] = me
        for v in writes:
            for t in v.tiles:
                t.w = me
                t.r = {}
        return [d for d in deps if d != me]

    def _mark(self, eng, deps):
        for d in deps:
            if d[0] == 'e':
                if d[1] == 'pe' and eng == 'pe':
                    continue
                self.ops[d[1]][d[2]]['signal'] = True

    def op(self, eng, emit, reads, writes):
        reads = [r for r in reads if isinstance(r, V)]
        writes = [w for w in writes if isinstance(w, V)]
        idx = len(self.ops[eng])
        me = ('e', eng, idx)
        deps = self._collect(reads, writes, me, eng)
        self.ops[eng].append(dict(emit=emit, deps=deps, signal=False))
        self._mark(eng, deps)

    def dma(self, q, out, in_, **kw):
        n = self.n_dma_q.get(q, 0)
        self.n_dma_q[q] = n + 1
        self.n_dma += 1
        s = n % self.NS
        val = 16 * (n // self.NS + 1)
        me = ('d', (q, s), val)
        reads = [in_] if isinstance(in_, V) else []
        writes = [out] if isinstance(out, V) else []
        self.dma_uid += 1
        deps = self._collect(reads, writes, me, ('d', self.dma_uid))
        if val > 16:
            deps.append(('d', (q, s), val - 16))
        o, i = _ap(out), _ap(in_)
        self.ops[q].append(dict(emit=lambda e: e.dma_start(out=o, in_=i, **kw), deps=deps,
                                signal=False, dma=((q, s), val)))
        self._mark(q, deps)

    def mm(self, out, lhsT, rhs, start=True, stop=True):
        o, l, r = _ap(out), _ap(lhsT), _ap(rhs)
        self.op('pe', lambda e: e.matmul(o, l, r, start=start, stop=stop), [lhsT, rhs], [out])

    def transpose(self, out, in_, ident):
        o, i, d = _ap(out), _ap(in_), _ap(ident)
        self.op('pe', lambda e: e.transpose(o, i, d), [in_, ident], [out])

    def act(self, out, in_, func, bias=None, scale=1.0, accum_out=None):
        o, i = _ap(out), _ap(in_)
        kw = {}
        if bias is not None:
            kw['bias'] = _ap(bias)
        kw['scale'] = _ap(scale)
        if accum_out is not None:
            kw['accum_out'] = _ap(accum_out)
        self.op('act', lambda e: e.activation(o, i, func, **kw), [in_, bias, scale],
                [out, accum_out])

    def tt(self, eng, out, in0, in1, op):
        o, a, b = _ap(out), _ap(in0), _ap(in1)
        self.op(eng, lambda e: e.tensor_tensor(o, a, b, op), [in0, in1], [out])

    def ts(self, eng, out, in0, s1, op0, s2=None, op1=None):
        o, a, x1, x2 = _ap(out), _ap(in0), _ap(s1), _ap(s2)
        if op1 is None:
            self.op(eng, lambda e: e.tensor_scalar(o, a, x1, None, op0), [in0, s1], [out])
        else:
            self.op(eng, lambda e: e.tensor_scalar(o, a, x1, x2, op0, op1), [in0, s1, s2], [out])

    def stt(self, eng, out, in0, scalar, in1, op0, op1):
        o, a, s, b = _ap(out), _ap(in0), _ap(scalar), _ap(in1)
        self.op(eng, lambda e: e.scalar_tensor_tensor(o, a, s, b, op0, op1), [in0, scalar, in1], [out])

    def copy(self, eng, out, in_):
        o, i = _ap(out), _ap(in_)
        if eng == 'act':
            self.op(eng, lambda e: e.copy(o, i), [in_], [out])
        else:
            self.op(eng, lambda e: e.tensor_copy(o, i), [in_], [out])

    def memset(self, eng, out, val):
        o = _ap(out)
        self.op(eng, lambda e: e.memset(o, val), [], [out])

    def scan(self, out, d0, d1, init, op0=ALU.mult, op1=ALU.add, eng='dve'):
        o, a, b, i = _ap(out), _ap(d0), _ap(d1), _ap(init)
        self.op(eng, lambda e: e.tensor_tensor_scan(o, a, b, i, op0, op1), [d0, d1, init], [out])

    def emit_all(self, block, esem, dsem):
        nc = self.nc
        final = []
        for q, nq in self.n_dma_q.items():
            for s in range(min(self.NS, nq)):
                cnt = (nq - 1 - s) // self.NS + 1
                final.append(('d', (q, s), 16 * cnt))
        self.ops['sp'].append(dict(emit=None, deps=final, signal=False))
        for e in ENG_NAMES:
            c = 0
            for r in self.ops[e]:
                if r['signal'] and 'dma' not in r:
                    c += 1
                    r['tick'] = c
        self.nwaits = {e: 0 for e in ENG_NAMES}

        def run(ename, eh):
            seen = {}
            for r in self.ops[ename]:
                need = {}
                for d in r['deps']:
                    if d[0] == 'e':
                        if d[1] == 'pe' and ename == 'pe':
                            continue
                        key = ('e', d[1])
                        val = self.ops[d[1]][d[2]]['tick']
                    else:
                        key = ('d', d[1])
                        val = d[2]
                    if seen.get(key, 0) >= val:
                        continue
                    if need.get(key, 0) < val:
                        need[key] = val
                for key, val in need.items():
                    sem = esem[key[1]] if key[0] == 'e' else dsem[key[1][0]][key[1][1]]
                    eh.wait_ge(sem, val)
                    seen[key] = val
                    self.nwaits[ename] += 1
                if r['emit'] is None:
                    continue
                ins = r['emit'](eh)
                if 'dma' in r:
                    ins.then_inc(dsem[r['dma'][0][0]][r['dma'][0][1]], 16)
                elif r['signal']:
                    ins.then_inc(esem[ename], 1)

        @block.tensor
        def _(e):
            run('pe', e)

        @block.scalar
        def _(e):
            run('act', e)

        @block.vector
        def _(e):
            run('dve', e)

        @block.gpsimd
        def _(e):
            run('pool', e)

        @block.sync
        def _(e):
            run('sp', e)


class Arena:
    PAGEW = 512

    def __init__(self, t, npages):
        self.t = t
        self.np = npages
        self.tiles = [Tile() for _ in range(npages)]
        self.used = [False] * npages
        self.ptr = 0

    def alloc(self, nbytes, top=False):
        n = (nbytes + 4 * self.PAGEW - 1) // (4 * self.PAGEW)
        if top:
            for s in range(self.np - n, -1, -1):
                if not any(self.used[s:s + n]):
                    for i in range(s, s + n):
                        self.used[i] = True
                    return (s, n)
            raise RuntimeError(f"arena full(top): need {n} pages, used {sum(self.used)}/{self.np}")
        start = self.ptr
        for off in range(self.np):
            s = (start + off) % self.np
            if s + n > self.np:
                continue
            if not any(self.used[s:s + n]):
                for i in range(s, s + n):
                    self.used[i] = True
                self.ptr = (s + n) % self.np
                return (s, n)
        raise RuntimeError(f"arena full: need {n} pages, used {sum(self.used)}/{self.np}")

    def free(self, h):
        s, n = h
        for i in range(s, s + n):
            assert self.used[i]
            self.used[i] = False

    def view(self, h, dt, shape=None, parts=128):
        s, n = h
        ap = self.t[0:parts, s * self.PAGEW:(s + n) * self.PAGEW]
        if dt != F32:
            ap = ap.bitcast(dt)
        v = V(ap, self.tiles[s:s + n])
        return v

import contextlib
from concourse.bass_utils import run_bass_kernel_spmd

D = 2048
T = 1024
NK = 16
DFF = 5632
NF = 44
EPS = 1e-6
ATT_SCALE = float(192 ** -0.5)
NPAGES = 62

W_SHAPES = dict(
    g_mix=(4, 2048), g_ffn=(4, 2048), g_final=(2048,), w_ada=(4, 2048, 12288), b_ada=(4, 12288),
    w_mla_in=(2, 2048, 1088), g_mla_q=(2, 512), g_mla_kv=(2, 512), w_mla_uq=(2, 512, 3072),
    w_mla_uk=(2, 512, 2048), w_mla_uv=(2, 512, 2048), w_mla_o=(2, 2048, 2048),
    w_rec_in=(2, 2048, 4096), w_rec_conv=(2, 4, 2048), b_rec_conv=(2, 2048),
    w_rec_gx=(2, 2, 16, 128, 128), b_rec_gx=(2, 2, 2048), w_rec_ga=(2, 2, 16, 128, 128),
    b_rec_ga=(2, 2, 2048), rec_lambda=(2, 2, 2048), w_rec_out=(2, 2048, 2048),
    w_ffn_up=(4, 2048, 11264), w_ffn_conv=(4, 3, 11264), b_ffn_conv=(4, 11264),
    w_ffn_down=(4, 5632, 2048),
)


def build_program(cfg=None):
    cfg = cfg or {}
    passes = cfg.get("passes", "PS")
    nlayers = cfg.get("nlayers", 4)
    nc = bass.Bass("TRN2", target_bir_lowering=False)
    dr = {}

    def din(name, shape):
        dr[name] = nc.dram_tensor(name, list(shape), F32, kind="ExternalInput").ap()

    din("xp", (1024, 2048)); din("xs", (1024, 2048)); din("cckv", (2, 256, 512)); din("ckr", (2, 256, 64))
    din("slru", (2, 2, 2048)); din("cvec", (2, 2048))
    for k_, s_ in W_SHAPES.items():
        din(k_, s_)
    din("ident", (128, 128)); din("ropec", (64, 1024)); din("ropes", (64, 1024))

    def dout(name, shape):
        dr[name] = nc.dram_tensor(name, list(shape), F32, kind="ExternalOutput").ap()

    dout("yp", (1024, 2048)); dout("ys", (1024, 2048)); dout("ockv", (4, 2, 256, 512))
    dout("okr", (4, 2, 256, 64)); dout("olru", (4, 2, 2, 2048))

    vec_srcs = [
        ("cvec", dr["cvec"].rearrange("c (k p) -> (c k) p", p=128), 32),
        ("g_mix", dr["g_mix"].rearrange("l (k p) -> (l k) p", p=128), 64),
        ("g_ffn", dr["g_ffn"].rearrange("l (k p) -> (l k) p", p=128), 64),
        ("g_final", dr["g_final"].rearrange("(k p) -> k p", p=128), 16),
        ("b_ada", dr["b_ada"].rearrange("l (k p) -> (l k) p", p=128), 384),
        ("g_mla_q", dr["g_mla_q"].rearrange("l (k p) -> (l k) p", p=128), 8),
        ("g_mla_kv", dr["g_mla_kv"].rearrange("l (k p) -> (l k) p", p=128), 8),
        ("w_rec_conv", dr["w_rec_conv"].rearrange("j t (k p) -> (j t k) p", p=128), 128),
        ("b_rec_conv", dr["b_rec_conv"].rearrange("j (k p) -> (j k) p", p=128), 32),
        ("b_rec_gx", dr["b_rec_gx"].rearrange("j d (k p) -> (j d k) p", p=128), 64),
        ("b_rec_ga", dr["b_rec_ga"].rearrange("j d (k p) -> (j d k) p", p=128), 64),
        ("rec_lambda", dr["rec_lambda"].rearrange("j d (k p) -> (j d k) p", p=128), 64),
        ("w_ffn_conv", dr["w_ffn_conv"].rearrange("l t (k p) -> (l t k) p", p=128), 1056),
        ("b_ffn_conv", dr["b_ffn_conv"].rearrange("l (k p) -> (l k) p", p=128), 352),
        ("slru", dr["slru"].rearrange("j d (k p) -> (j d k) p", p=128), 64),
    ]
    voff = {}
    o_ = 0
    for name, _, r_ in vec_srcs:
        voff[name] = o_
        o_ += r_
    NVEC = o_
    der = {}
    for name, n_ in [("mod", 768), ("a1", 128), ("a2", 128), ("cph", 64), ("cp", 64), ("hbgx", 64),
                     ("hbga", 64), ("one", 1), ("st", 256)]:
        der[name] = o_
        o_ += n_
    NCST = o_

    with contextlib.ExitStack() as es:
        xb_t = es.enter_context(nc.sbuf_tensor("xb", [128, NK, T], F32))
        ar_t = es.enter_context(nc.sbuf_tensor("arena", [128, NPAGES * 512], F32))
        cst_t = es.enter_context(nc.sbuf_tensor("cst", [128, NCST], F32))
        id_t = es.enter_context(nc.sbuf_tensor("identf", [128, 128], F32))
        ones_t = es.enter_context(nc.sbuf_tensor("onesb", [128, 128], BF16))
        sc_t = es.enter_context(nc.sbuf_tensor("scb", [128, 32], BF16))
        ps_t = es.enter_context(nc.psum_tensor("ps", [128, 8 * 512], F32))
        esem = {e: es.enter_context(nc.semaphore("e_" + e)) for e in ENG_NAMES}
        NDS = 24
        dsem = {q: [es.enter_context(nc.semaphore(f"d{q}{i}")) for i in range(NDS)] for q in ("sp", "pool")}
        block = es.enter_context(nc.Block())

        P = Prog(nc, n_dma_sems=NDS)
        A = Arena(ar_t, NPAGES)
        xtiles = [Tile() for _ in range(NK)]
        pstiles = [Tile() for _ in range(8)]
        ps_used = [False] * 8
        ps_ptr = [0]

        def XB(k0, k1=None, t0=0, t1=T):
            if k1 is None:
                return V(xb_t[:, k0, t0:t1], [xtiles[k0]])
            return V(xb_t[:, k0:k1, t0:t1], xtiles[k0:k1])

        def ps_alloc(n=1):
            start = ps_ptr[0]
            for off in range(8):
                s = (start + off) % 8
                if n == 2 and s % 2:
                    continue
                if s + n > 8 or any(ps_used[s:s + n]):
                    continue
                for i in range(s, s + n):
                    ps_used[i] = True
                ps_ptr[0] = (s + n) % 8
                return (s, n)
            raise RuntimeError("psum full")

        def ps_free(h):
            for i in range(h[0], h[0] + h[1]):
                assert ps_used[i]
                ps_used[i] = False

        def PSv(h, parts=128):
            s, n = h
            return V(ps_t[0:parts, s * 512:(s + n) * 512], pstiles[s:s + n])

        def PSb(h, i, parts=128, n=512):
            s = h[0] + i
            return V(ps_t[0:parts, s * 512:s * 512 + n], [pstiles[s]])

        vt = {name: Tile() for name in list(voff) + list(der)}

        per_layer = {"mod": 192, "a1": 32, "a2": 32}
        vtl = {nm: [Tile() for _ in range(4)] for nm in per_layer}

        def vcol(name, i, n=1):
            base = voff[name] if name in voff else der[name]
            if name in per_layer:
                return V(cst_t[:, base + i:base + i + n], [vtl[name][i // per_layer[name]]])
            return V(cst_t[:, base + i:base + i + n], [vt[name]])

        ident = V(id_t[:, :], [Tile()])
        ones = V(ones_t[:, :], [Tile()])
        scb = V(sc_t[:, :], [Tile()])

        def abuf(nbytes, dt, parts=128, top=False):
            h = A.alloc(nbytes, top=top)
            return h, A.view(h, dt, parts=parts)

        evac_rr = [0]
        EW2 = cfg.get("ew2", "dve")

        P.dma('sp', ident, dr["ident"])
        P.memset('dve', ones, 1.0)
        P.memset('dve', vcol("one", 0), 1.0)
        for name, src, R in vec_srcs:
            for r0 in range(0, R, 128):
                rr = min(128, R - r0)
                h, stg = abuf(512, F32)
                stg_r = V(stg.ap[0:rr, 0:128], stg.tiles)
                P.dma('sp', stg_r, src[r0:r0 + rr, :])
                ph = ps_alloc(1)
                pv = PSb(ph, 0, n=rr)
                P.transpose(pv, stg_r, V(ident.ap[0:rr, 0:rr], ident.tiles))
                P.copy('dve', vcol(name, r0, rr), pv)
                ps_free(ph)
                A.free(h)
        P.act(scb, vcol("cvec", 0, 32), AF.Silu)
        h, tmpv = abuf(512, F32)
        e1 = tmpv[:, 0:64]
        P.act(e1, vcol("rec_lambda", 0, 64), AF.Exp, scale=-1.0)
        P.ts('dve', e1, e1, 1.0, ALU.add)
        P.act(e1, e1, AF.Ln)
        P.ts('dve', vcol("cph", 0, 64), e1, -4.0, ALU.mult)
        P.ts('dve', vcol("cp", 0, 64), e1, -8.0, ALU.mult)
        A.free(h)
        P.ts('dve', vcol("hbgx", 0, 64), vcol("b_rec_gx", 0, 64), 0.5, ALU.mult)
        P.ts('dve', vcol("hbga", 0, 64), vcol("b_rec_ga", 0, 64), 0.5, ALU.mult)
        scv = scb.rearrange("p (c k) -> p k c", c=2)
        ada_pending = {}

        ada_todo = [(l, jb) for l in range(4) for jb in range(96)]
        ada_ready = []
        ada_s2 = []

        def ada_prep():
            if not ada_todo:
                return
            l_, jb = ada_todo.pop(0)
            h, wv = abuf(16 * 128 * 2, BF16)
            w3 = wv.rearrange("p (k c) -> p k c", k=16)
            P.dma('pool', w3, dr["w_ada"][l_].rearrange("(k p) n -> p k n", p=128)[:, :, jb * 128:(jb + 1) * 128])
            ada_ready.append((l_, jb, h, w3))

        def ada_stage2():
            items, h2, row = ada_s2.pop(0)
            ph2 = ps_alloc(1)
            for i in range(len(items)):
                P.transpose(PSb(ph2, 0)[:, 2 * i:2 * i + 2], row[:, i * 128:(i + 1) * 128],
                            V(ident.ap[0:2, 0:2], ident.tiles))
            A.free(h2)
            i = 0
            while i < len(items):
                l, jb0 = items[i]
                cnt = 1
                while i + cnt < len(items) and items[i + cnt] == (l, jb0 + cnt):
                    cnt += 1
                src = PSb(ph2, 0)[:, 2 * i:2 * (i + cnt)].rearrange("p (j c) -> p j c", c=2)
                for c in range(2):
                    P.copy('dve', vcol("mod", (l * 2 + c) * 96 + jb0, cnt), src[:, :, c])
                if jb0 + cnt == 96:
                    for c in range(2):
                        P.tt('dve', vcol("mod", (l * 2 + c) * 96, 96), vcol("mod", (l * 2 + c) * 96, 96),
                             vcol("b_ada", l * 96, 96), ALU.add)
                        mb = (l * 2 + c) * 96
                        P.stt('dve', vcol("a1", (l * 2 + c) * 16, 16), vcol("mod", mb + 16, 16), 1.0,
                              vcol("g_mix", l * 16, 16), ALU.add, ALU.mult)
                        P.stt('dve', vcol("a2", (l * 2 + c) * 16, 16), vcol("mod", mb + 64, 16), 1.0,
                              vcol("g_ffn", l * 16, 16), ALU.add, ALU.mult)
                i += cnt
            ps_free(ph2)

        def ada_stage1():
            while ada_ready:
                batch = ada_ready[:4]
                del ada_ready[:4]
                ph = ps_alloc(1)
                for i, (l, jb, h, w3) in enumerate(batch):
                    for k in range(NK):
                        P.mm(PSb(ph, 0, parts=2)[:, i * 128:(i + 1) * 128], scv[:, k, :], w3[:, k, :],
                             start=(k == 0), stop=(k == NK - 1))
                    A.free(h)
                h2, row = abuf(2048, F32, parts=2)
                n_ = len(batch) * 128
                P.copy('act', row[:, 0:n_], PSb(ph, 0, parts=2, n=n_))
                ps_free(ph)
                ada_s2.append(([(b[0], b[1]) for b in batch], h2, row))

        def ada_tick(n_items):
            if not (ada_todo or ada_ready or ada_s2):
                return
            while ada_s2:
                ada_stage2()
            ada_stage1()
            for _ in range(n_items):
                ada_prep()

        def ada_flush(upto):
            while ada_todo and ada_todo[0][0] <= upto:
                ada_tick(4)
            if any(r[0] <= upto for r in ada_ready) or any(it[0][0][0] <= upto for it in ada_s2):
                ada_tick(0)
                ada_tick(0)

        def modcol(l, c, which, k):
            return vcol("mod", (l * 2 + c) * 96 + which * 16 + k)

        def load_x(src):
            for tb in range(8):
                h, stg = abuf(8192, F32)
                P.dma('sp', stg, src[tb * 128:(tb + 1) * 128, :])
                for kg in range(4):
                    ph = ps_alloc(1)
                    for kk in range(4):
                        k = kg * 4 + kk
                        P.transpose(PSb(ph, 0)[:, kk * 128:(kk + 1) * 128], stg[:, k * 128:(k + 1) * 128], ident)
                    dst = XB(kg * 4, kg * 4 + 4, tb * 128, (tb + 1) * 128)
                    srcv = PSb(ph, 0).rearrange("p (a b) -> p a b", a=4)
                    P.copy('act' if kg % 2 else 'dve', dst, srcv)
                    ps_free(ph)
                A.free(h)

        def rstd_of(chunks, dim, use_pre=False):
            hr, rs = abuf(4096, F32)
            pre = pend_stats.pop(0) if (use_pre and pend_stats) else None
            for half in range(2):
                if pre is not None:
                    ph = (pre[0] + half, 1)
                else:
                    ph = ps_alloc(1)
                    n = len(chunks)
                    for k, ch in enumerate(chunks):
                        hs, sq = abuf(1024, BF16)
                        sq = sq[:, 0:512]
                        P.act(sq, ch[:, half * 512:(half + 1) * 512], AF.Square)
                        P.mm(PSb(ph, 0), ones, sq, start=(k == 0), stop=(k == n - 1))
                        A.free(hs)
                rsh = rs[:, half * 512:(half + 1) * 512]
                P.ts('dve', rsh, PSb(ph, 0), 1.0 / dim, ALU.mult, EPS, ALU.add)
                if pre is None:
                    ps_free(ph)
            if pre is not None:
                ps_free(pre)
            for half in range(2):
                rsh = rs[:, half * 512:(half + 1) * 512]
                P.act(rsh, rsh, AF.Sqrt)
                P.op('dve', (lambda e, o=rsh.ap: e.reciprocal(o, o)), [rsh], [rsh])
            return hr, rs

        def norm_mod(acol, shcol):
            hr, rs = rstd_of([XB(k) for k in range(NK)], D, use_pre=True)
            hb = []
            for k in range(NK):
                ht, tmp = abuf(4096, F32)
                P.tt('dve', tmp, XB(k), rs, ALU.mult)
                hh, hv = abuf(2048, BF16, top=True)
                P.act(hv, tmp, AF.Identity, bias=shcol(k), scale=acol(k))
                A.free(ht)
                hb.append((hh, hv))
            A.free(hr)
            return hb

        pend_stats = []

        def proj_residual(w_src_fn, nk_in, rhs_chunks, gcol, stats=False):
            def load(m):
                hw, wv = abuf(nk_in * 256, BF16)
                w3 = wv[:, 0:nk_in * 128].rearrange("p (k c) -> p k c", k=nk_in)
                P.dma('pool', w3, w_src_fn(m))
                return hw, w3
            nxt = load(0)
            sph = ps_alloc(2) if stats else None
            sq_pend = []

            def stats_mm(sph_):
                m_, sqs_ = sq_pend.pop(0)
                for half in range(2):
                    P.mm(PSb(sph_, half), ones, sqs_[half][1], start=(m_ == 0), stop=(m_ == NK - 1))
                    A.free(sqs_[half][0])
            for m in range(NK):
                hw, w3 = nxt
                if m + 1 < NK:
                    nxt = load(m + 1)
                ph = ps_alloc(2)
                for k in range(nk_in):
                    for half in range(2):
                        P.mm(PSb(ph, half), w3[:, k, :], rhs_chunks[k][:, half * 512:(half + 1) * 512],
                             start=(k == 0), stop=(k == nk_in - 1))
                A.free(hw)
                P.stt('dve', XB(m), PSv(ph), gcol(m), XB(m), ALU.mult, ALU.add)
                ps_free(ph)
                if stats:
                    sqs = []
                    for half in range(2):
                        hs, sq = abuf(1024, BF16)
                        sq = sq[:, 0:512]
                        P.act(sq, XB(m)[:, half * 512:(half + 1) * 512], AF.Square)
                        sqs.append((hs, sq))
                    sq_pend.append((m, sqs))
                    while len(sq_pend) > 2:
                        stats_mm(sph)
            if stats:
                while sq_pend:
                    stats_mm(sph)
                pend_stats.append(sph)

        def ffn(l, c, segs):
            hb = norm_mod(lambda k: vcol("a2", (l * 2 + c) * 16 + k), lambda k: modcol(l, c, 3, k))
            L = T // segs
            wup = dr["w_ffn_up"][l].rearrange("(k p) n -> p k n", p=128)
            wdn = dr["w_ffn_down"][l].rearrange("(f p) n -> p f n", p=128)

            def load_up(f):
                hw, wv = abuf(16 * 256 * 2, BF16)
                w3 = wv.rearrange("p (k c) -> p k c", k=16)
                P.dma('pool', w3[:, :, 0:128], wup[:, :, f * 128:(f + 1) * 128])
                P.dma('pool', w3[:, :, 128:256], wup[:, :, DFF + f * 128: DFF + (f + 1) * 128])
                return hw, w3
            nxt = load_up(0)
            for (f0, nf) in [(0, 11), (11, 11), (22, 11), (33, 11)]:
                ab = []
                for f in range(f0, f0 + nf):
                    hw, w3 = nxt
                    if f + 1 < NF:
                        nxt = load_up(f + 1)
                    us = []
                    for part in range(2):
                        ch = f + part * NF
                        ph = ps_alloc(2)
                        for k in range(NK):
                            for half in range(2):
                                P.mm(PSb(ph, half), w3[:, k, part * 128:(part + 1) * 128],
                                     hb[k][1][:, half * 512:(half + 1) * 512], start=(k == 0), stop=(k == NK - 1))
                        hu, u = abuf(4096, F32)
                        pv = PSv(ph)
                        P.act(u, pv, AF.Identity, bias=vcol("b_ffn_conv", l * 88 + ch),
                              scale=vcol("w_ffn_conv", (l * 3 + 1) * 88 + ch))
                        u3 = u.rearrange("p (s t) -> p s t", s=segs)
                        p3 = pv.rearrange("p (s t) -> p s t", s=segs)
                        P.stt('dve', u3[:, :, 1:], p3[:, :, :-1], vcol("w_ffn_conv", (l * 3 + 0) * 88 + ch),
                              u3[:, :, 1:], ALU.mult, ALU.add)
                        P.stt('dve', u3[:, :, :-1], p3[:, :, 1:], vcol("w_ffn_conv", (l * 3 + 2) * 88 + ch),
                              u3[:, :, :-1], ALU.mult, ALU.add)
                        ps_free(ph)
                        us.append((hu, u))
                    A.free(hw)
                    hg, sg = abuf(4096, F32)
                    P.act(sg, us[0][1], AF.Silu)
                    ha, av = abuf(2048, BF16, top=True)
                    P.tt('dve', av, sg, us[1][1], ALU.mult)
                    A.free(hg); A.free(us[0][0]); A.free(us[1][0])
                    ab.append((ha, av))
                    ada_tick(1)
                proj_residual(lambda m: wdn[:, f0:f0 + nf, m * 128:(m + 1) * 128], nf, [a[1] for a in ab],
                              lambda m: modcol(l, c, 5, m), stats=(f0 + nf == NF))
                for a in ab:
                    A.free(a[0])
            for hh, _ in hb:
                A.free(hh)

        def final_norm(dst):
            hr, rs = rstd_of([XB(k) for k in range(NK)], D, use_pre=True)
            for k in range(NK):
                P.stt('dve', XB(k), XB(k), vcol("g_final", k), rs, ALU.mult, ALU.mult)
            A.free(hr)
            for tb in range(8):
                h, stg = abuf(8192, F32)
                for kg in range(4):
                    ph = ps_alloc(1)
                    for kk in range(4):
                        k = kg * 4 + kk
                        P.transpose(PSb(ph, 0)[:, kk * 128:(kk + 1) * 128], XB(k, None, tb * 128, (tb + 1) * 128), ident)
                    P.copy('act' if kg % 2 else 'dve', stg[:, kg * 512:(kg + 1) * 512], PSb(ph, 0))
                    ps_free(ph)
                P.dma('sp', dst[tb * 128:(tb + 1) * 128, :], stg)
                A.free(h)

        def rec(l, c, S):
            j = l // 2
            segs = 1 if S else 4
            L = T // segs
            hb = norm_mod(lambda k: vcol("a1", (l * 2 + c) * 16 + k), lambda k: modcol(l, c, 0, k))
            win = dr["w_rec_in"][j].rearrange("(k p) n -> p k n", p=128)
            mixed = []
            stA = {}

            def stage_A(n):
                hw, wv = abuf(16 * 256 * 2, BF16)
                w3 = wv.rearrange("p (k c) -> p k c", k=16)
                P.dma('pool', w3[:, :, 0:128], win[:, :, n * 128:(n + 1) * 128])
                P.dma('pool', w3[:, :, 128:256], win[:, :, D + n * 128: D + (n + 1) * 128])
                hgw, gwv = abuf(4 * 128 * 2, BF16, top=True)
                gw = gwv[:, 0:512].rearrange("p (a o) -> p a o", a=4)
                for d in range(2):
                    P.dma('pool', gw[:, d * 2 + 0, :], dr["w_rec_gx"][j, d, n])
                    P.dma('pool', gw[:, d * 2 + 1, :], dr["w_rec_ga"][j, d, n])
                phy = ps_alloc(2)
                phx = ps_alloc(2)
                for part, ph in ((0, phy), (1, phx)):
                    for k in range(NK):
                        for half in range(2):
                            P.mm(PSb(ph, half), w3[:, k, part * 128:(part + 1) * 128],
                                 hb[k][1][:, half * 512:(half + 1) * 512], start=(k == 0), stop=(k == NK - 1))
                A.free(hw)
                hy, y32 = abuf(4096, F32)
                P.copy('act', y32, PSv(phy))
                ps_free(phy)
                ht1, t1 = abuf(4096, F32, top=True)
                P.tt(EW2, t1, y32, y32, ALU.mult)
                P.ts(EW2, t1, t1, 0.044715, ALU.mult, 1.0, ALU.add)
                P.tt(EW2, t1, t1, y32, ALU.mult)
                P.act(t1, t1, AF.Tanh, scale=0.7978845608028654)
                P.stt('dve', t1, t1, 1.0, y32, ALU.add, ALU.mult)
                A.free(hy)
                hxc, xc = abuf(4096, F32, top=True)
                pv = PSv(phx)
                cw = lambda t: vcol("w_rec_conv", (j * 4 + t) * 16 + n)
                P.act(xc, pv, AF.Identity, bias=vcol("b_rec_conv", j * 16 + n), scale=cw(2))
                x3 = xc.rearrange("p (s t) -> p s t", s=segs)
                p3 = pv.rearrange("p (s t) -> p s t", s=segs)
                P.stt('dve', x3[:, :, 2:], p3[:, :, :-2], cw(0), x3[:, :, 2:], ALU.mult, ALU.add)
                P.stt('dve', x3[:, :, 1:], p3[:, :, :-1], cw(1), x3[:, :, 1:], ALU.mult, ALU.add)
                P.stt('dve', x3[:, :, :-1], p3[:, :, 1:], cw(3), x3[:, :, :-1], ALU.mult, ALU.add)
                ps_free(phx)
                hx16, xc16 = abuf(2048, BF16, top=True)
                P.copy('act', xc16, xc)
                stA[n] = dict(ht1=ht1, t1=t1, hxc=hxc, xc=xc, hx16=hx16, xc16=xc16, hgw=hgw, gw=gw)

            def stage_B(n):
                sa = stA.pop(n)
                xc, xc16, gw, t1 = sa['xc'], sa['xc16'], sa['gw'], sa['t1']
                dd = []
                for d in range(2):
                    cidx = (j * 2 + d) * 16 + n
                    pgx = ps_alloc(2)
                    pga = ps_alloc(2)
                    for half in range(2):
                        P.mm(PSb(pgx, half), gw[:, d * 2 + 0, :], xc16[:, half * 512:(half + 1) * 512])
                        P.mm(PSb(pga, half), gw[:, d * 2 + 1, :], xc16[:, half * 512:(half + 1) * 512])
                    hgxt, gx = abuf(4096, F32)
                    hgat, ga = abuf(4096, F32)
                    P.act(gx, PSv(pgx), AF.Tanh, bias=vcol("hbgx", cidx), scale=0.5)
                    P.act(ga, PSv(pga), AF.Tanh, bias=vcol("hbga", cidx), scale=0.5)
                    ps_free(pgx); ps_free(pga)
                    hat, a = abuf(4096, F32)
                    P.act(a, ga, AF.Exp, bias=vcol("cph", cidx), scale=vcol("cph", cidx))
                    P.act(ga, ga, AF.Exp, bias=vcol("cp", cidx), scale=vcol("cp", cidx))
                    P.stt('dve', gx, gx, 1.0, xc, ALU.add, ALU.mult)
                    dd.append(dict(gx=gx, hgx=hgxt, ga=ga, hga=hgat, a=a, ha=hat, cidx=cidx))
                A.free(sa['hx16']); A.free(sa['hxc']); A.free(sa['hgw'])
                for x_ in dd:
                    P.act(x_['ga'], x_['ga'], AF.Sqrt, bias=vcol("one", 0), scale=-1.0)
                hs = []
                for d, x_ in enumerate(dd):
                    gx, ga, a, cidx = x_['gx'], x_['ga'], x_['a'], x_['cidx']
                    P.stt('dve', gx, gx, 0.5, ga, ALU.mult, ALU.mult)
                    A.free(x_['hga'])
                    hh, hd = abuf(4096, F32)
                    for s in range(segs):
                        sl = slice(s * L, (s + 1) * L)
                        init = vcol("slru", cidx) if S else 0.0
                        if d == 0:
                            P.scan(hd[:, sl], a[:, sl], gx[:, sl], init)
                        else:
                            P.scan(hd[:, sl][:, ::-1], a[:, sl][:, ::-1], gx[:, sl][:, ::-1], init)
                    A.free(x_['hgx']); A.free(x_['ha'])
                    if not S:
                        h3 = hd.rearrange("p (s t) -> p s t", s=4)
                        stv = V(cst_t[:, der["st"] + j * 128: der["st"] + (j + 1) * 128], [vt["st"]]) \
                            .rearrange("p (s d n) -> p s d n", s=4, d=2)
                        P.copy('dve', stv[:, :, d, n], h3[:, :, L - 1] if d == 0 else h3[:, :, 0])
                    hs.append((hh, hd))
                P.tt(EW2, hs[0][1], hs[0][1], hs[1][1], ALU.add)
                hm, mv = abuf(2048, BF16, top=True)
                P.stt('dve', mv, hs[0][1], 0.5, t1, ALU.mult, ALU.mult)
                A.free(hs[0][0]); A.free(hs[1][0]); A.free(sa['ht1'])
                mixed.append((hm, mv))

            for n in range(NK + 1):
                if n < NK:
                    stage_A(n)
                    ada_tick(3)
                if n >= 1:
                    stage_B(n - 1)
            for hh, _ in hb:
                A.free(hh)
            wout = dr["w_rec_out"][j].rearrange("(k p) n -> p k n", p=128)
            proj_residual(lambda m: wout[:, :, m * 128:(m + 1) * 128], NK, [m_[1] for m_ in mixed],
                          lambda m: modcol(l, c, 2, m), stats=cfg.get("ffn", True))
            for m_ in mixed:
                A.free(m_[0])
            if not S:
                ph = ps_alloc(1)
                stsrc = V(cst_t[:, der["st"] + j * 128: der["st"] + (j + 1) * 128], [vt["st"]])
                P.transpose(PSb(ph, 0, n=128), stsrc, ident)
                h, stg = abuf(512, F32)
                P.copy('dve', stg[:, 0:128], PSb(ph, 0, n=128))
                ps_free(ph)
                for s in range(4):
                    P.dma('sp', dr["olru"][s, j].rearrange("d (n p) -> (d n) p", p=128),
                          V(stg.ap[s * 32:(s + 1) * 32, 0:128], stg.tiles))
                A.free(h)

        def mla(l, c, S):
            j = l // 2
            KOFF = 256 if S else 0
            TK = KOFF + T
            NKC = TK // 128
            hb = norm_mod(lambda k: vcol("a1", (l * 2 + c) * 16 + k), lambda k: modcol(l, c, 0, k))
            win = dr["w_mla_in"][j].rearrange("(k p) n -> p k n", p=128)
            PERM = [(0, 16), (16, 0), (32, 48), (48, 32)]
            cq = []
            ckv = []
            hkr = hkrp = None
            NMC = 10 if S else 9

            def load_in(mc):
                hw, wv = abuf(16 * 128 * 2, BF16)
                w3 = wv.rearrange("p (k c) -> p k c", k=16)
                if mc < 8:
                    P.dma('pool', w3, win[:, :, mc * 128:(mc + 1) * 128])
                elif mc == 8:
                    P.dma('pool', w3[:, :, 0:64], win[:, :, 1024:1088])
                else:
                    for (dc, sc_) in PERM:
                        P.dma('pool', w3[:, :, dc:dc + 16], win[:, :, 1024 + sc_:1024 + sc_ + 16])
                return hw, w3
            nxt_in = load_in(0)
            for mc in range(NMC):
                hw, w3 = nxt_in
                if mc + 1 < NMC:
                    nxt_in = load_in(mc + 1)
                M = 128 if mc < 8 else 64
                ph = ps_alloc(2)
                for k in range(NK):
                    for half in range(2):
                        P.mm(PSb(ph, half, parts=M), w3[:, k, 0:M], hb[k][1][:, half * 512:(half + 1) * 512],
                             start=(k == 0), stop=(k == NK - 1))
                A.free(hw)
                ho, ov = abuf(4096, F32, parts=M, top=True)
                P.copy('act', ov, PSv(ph, parts=M))
                ps_free(ph)
                if mc < 4:
                    cq.append((ho, ov))
                elif mc < 8:
                    ckv.append((ho, ov))
                elif mc == 8:
                    hkr, kr32 = ho, ov
                else:
                    hkrp, krp32 = ho, ov
            for hh, _ in hb:
                A.free(hh)
            hr, rs = rstd_of([x_[1] for x_ in cq], 512)
            qlat = []
            for m in range(4):
                hq, qv = abuf(2048, BF16, top=True)
                P.stt('dve', qv, cq[m][1], vcol("g_mla_q", j * 4 + m), rs, ALU.mult, ALU.mult)
                A.free(cq[m][0])
                qlat.append((hq, qv))
            A.free(hr)
            hr, rs = rstd_of([x_[1] for x_ in ckv], 512)
            hck, ck16 = abuf(4 * TK * 2, BF16, top=True)
            ck3 = ck16[:, 0:4 * TK].rearrange("p (m t) -> p m t", m=4)
            for m in range(4):
                P.stt('dve', ckv[m][1], ckv[m][1], vcol("g_mla_kv", j * 4 + m), rs, ALU.mult, ALU.mult)
                P.copy('act', ck3[:, m, KOFF:TK], ckv[m][1])
            A.free(hr)
            hk16, kr16f = abuf(TK * 2, BF16, parts=64, top=True)
            kr16 = kr16f[:, 0:TK]
            if not S:
                for tb in range(8):
                    s, half = tb // 2, tb % 2
                    ph = ps_alloc(1)
                    for m in range(4):
                        P.transpose(PSb(ph, 0)[:, m * 128:(m + 1) * 128], ckv[m][1][:, tb * 128:(tb + 1) * 128], ident)
                    h, stg = abuf(2048, F32)
                    P.copy('dve', stg, PSb(ph, 0))
                    ps_free(ph)
                    P.dma('sp', dr["ockv"][s, j, half * 128:(half + 1) * 128, :], stg)
                    A.free(h)
                ph = ps_alloc(1)
                for tb in range(8):
                    P.transpose(PSb(ph, 0)[:, tb * 64:(tb + 1) * 64], kr32[:, tb * 128:(tb + 1) * 128],
                                V(ident.ap[0:64, 0:64], ident.tiles))
                h, stg = abuf(2048, F32)
                P.copy('dve', stg, PSb(ph, 0))
                ps_free(ph)
                for tb in range(8):
                    s, half = tb // 2, tb % 2
                    P.dma('sp', dr["okr"][s, j, half * 128:(half + 1) * 128, :], stg[:, tb * 64:(tb + 1) * 64])
                A.free(h)
                P.copy('act', kr16, kr32)
            else:
                h, stg = abuf(4096, F32)
                st3 = stg.rearrange("p (a f) -> p a f", a=2)
                for a in range(2):
                    P.dma('sp', st3[:, a, :], dr["cckv"][j, a * 128:(a + 1) * 128, :])
                for m in range(4):
                    ph = ps_alloc(1)
                    for a in range(2):
                        P.transpose(PSb(ph, 0)[:, a * 128:(a + 1) * 128], st3[:, a, m * 128:(m + 1) * 128], ident)
                    P.copy('dve', ck3[:, m, 0:256], PSb(ph, 0)[:, 0:256])
                    ps_free(ph)
                A.free(h)
                h, stg = abuf(512, F32)
                st3 = stg[:, 0:128].rearrange("p (a f) -> p a f", a=2)
                for a in range(2):
                    P.dma('sp', st3[:, a, :], dr["ckr"][j, a * 128:(a + 1) * 128, :])
                ph = ps_alloc(1)
                for a in range(2):
                    P.transpose(PSb(ph, 0, parts=64)[:, a * 128:(a + 1) * 128], st3[:, a, :], ident)
                P.copy('dve', kr16[:, 0:256], PSb(ph, 0, parts=64)[:, 0:256])
                ps_free(ph)
                A.free(h)
                hcc, cc = abuf(4096, F32, parts=64, top=True)
                hss, ss = abuf(4096, F32, parts=64, top=True)
                P.dma('sp', cc, dr["ropec"])
                P.dma('sp', ss, dr["ropes"])
                P.tt('dve', kr32, kr32, cc, ALU.mult)
                P.tt('dve', krp32, krp32, ss, ALU.mult)
                P.tt('dve', kr16[:, 256:TK], kr32, krp32, ALU.add)
                A.free(hkrp)
            A.free(hkr)
            for m in range(4):
                A.free(ckv[m][0])
            ao = [abuf(2048, BF16, top=True) for _ in range(NK)]
            wuq = dr["w_mla_uq"][j].rearrange("(k p) n -> p k n", p=128)
            wuk = dr["w_mla_uk"][j].rearrange("(k p) n -> p k n", p=128)
            wuv = dr["w_mla_uv"][j].rearrange("(k p) n -> p k n", p=128)
            if S:
                probs = [(0, 512, list(range(NKC))), (512, 512, list(range(NKC)))]
            else:
                probs = [(s * 256, 256, [2 * s, 2 * s + 1]) for s in range(4)]
            units = [(h_, pi, ki) for h_ in range(16) for pi in range(len(probs)) for ki in range(len(probs[pi][2]))]
            hd = {}

            hw_pre = {}

            def head_load(h_):
                hq_, wq = abuf(4 * 256 * 2, BF16)
                wq3 = wq.rearrange("p (k c) -> p k c", k=4)
                P.dma('pool', wq3[:, :, 0:192], wuq[:, :, h_ * 192:(h_ + 1) * 192])
                if S:
                    for (dc, sc_) in PERM:
                        P.dma('pool', wq3[:, :, 192 + dc:192 + dc + 16],
                              wuq[:, :, h_ * 192 + 128 + sc_: h_ * 192 + 128 + sc_ + 16])
                hk_, wk = abuf(4 * 128 * 2 * 2, BF16)
                wk3 = wk.rearrange("p (a k c) -> p a k c", a=2, k=4)
                P.dma('pool', wk3[:, 0], wuk[:, :, h_ * 128:(h_ + 1) * 128])
                P.dma('pool', wk3[:, 1], wuv[:, :, h_ * 128:(h_ + 1) * 128])
                hw_pre[h_] = (hq_, wq3, hk_, wk3)

            def head_prep(h_):
                if h_ not in hw_pre:
                    head_load(h_)
                hq_, wq3, hk_, wk3 = hw_pre.pop(h_)
                if h_ + 1 < 16:
                    head_load(h_ + 1)
                hqn, qn = abuf(2048, BF16)
                ph = ps_alloc(2)
                for k in range(4):
                    for half in range(2):
                        P.mm(PSb(ph, half), wq3[:, k, 0:128], qlat[k][1][:, half * 512:(half + 1) * 512],
                             start=(k == 0), stop=(k == 3))
                P.copy('act', qn, PSv(ph))
                ps_free(ph)
                hqr, qr = abuf(2048, BF16, parts=64)
                ph = ps_alloc(2)
                for k in range(4):
                    for half in range(2):
                        P.mm(PSb(ph, half, parts=64), wq3[:, k, 128:192], qlat[k][1][:, half * 512:(half + 1) * 512],
                             start=(k == 0), stop=(k == 3))
                if S:
                    ht1, t1 = abuf(4096, F32, parts=64)
                    ht2, t2 = abuf(4096, F32, parts=64)
                    P.tt('dve', t1, PSv(ph, parts=64), cc, ALU.mult)
                    ps_free(ph)
                    ph2 = ps_alloc(2)
                    for k in range(4):
                        for half in range(2):
                            P.mm(PSb(ph2, half, parts=64), wq3[:, k, 192:256],
                                 qlat[k][1][:, half * 512:(half + 1) * 512], start=(k == 0), stop=(k == 3))
                    P.tt('dve', t2, PSv(ph2, parts=64), ss, ALU.mult)
                    ps_free(ph2)
                    P.tt('dve', qr, t1, t2, ALU.add)
                    A.free(ht1); A.free(ht2)
                else:
                    P.copy('act', qr, PSv(ph, parts=64))
                    ps_free(ph)
                A.free(hq_)
                hkn, knf = abuf(TK * 2, BF16)
                kn = knf[:, 0:TK]
                for t0 in range(0, TK, 512):
                    n_ = min(512, TK - t0)
                    ph = ps_alloc(1)
                    for k in range(4):
                        P.mm(PSb(ph, 0, n=n_), wk3[:, 0, k, :], ck3[:, k, t0:t0 + n_], start=(k == 0), stop=(k == 3))
                    P.copy('act', kn[:, t0:t0 + n_], PSb(ph, 0, n=n_))
                    ps_free(ph)
                hv_, vf = abuf(NKC * 128 * 2, BF16)
                v3 = vf[:, 0:NKC * 128].rearrange("p (a c) -> p a c", a=NKC)
                for t0 in range(0, NKC, 4):
                    n_ = min(4, NKC - t0)
                    ph = ps_alloc(1)
                    for i in range(n_):
                        tcn = t0 + i
                        for k in range(4):
                            P.mm(PSb(ph, 0)[:, i * 128:(i + 1) * 128], ck3[:, k, tcn * 128:(tcn + 1) * 128],
                                 wk3[:, 1, k, :], start=(k == 0), stop=(k == 3))
                    P.copy('dve', v3[:, t0:t0 + n_, :], PSb(ph, 0, n=n_ * 128).rearrange("p (a c) -> p a c", a=n_))
                    ps_free(ph)
                A.free(hk_)
                hd[h_] = dict(qn=qn, qr=qr, kn=kn, v3=v3, hs=[hqn, hqr, hkn, hv_])

            pend = {}

            def emit_S(i):
                h_, pi, ki = units[i]
                if pi == 0 and ki == 0:
                    head_prep(h_)
                    ada_tick(3)
                q0, nq, kcs = probs[pi]
                kc = kcs[ki]
                x = hd[h_]
                ph = ps_alloc(1)
                sp_ = PSb(ph, 0, n=nq)
                P.mm(sp_, x['kn'][:, kc * 128:(kc + 1) * 128], x['qn'][:, q0:q0 + nq], start=True, stop=False)
                P.mm(sp_, kr16[:, kc * 128:(kc + 1) * 128], x['qr'][:, q0:q0 + nq], start=False, stop=True)
                he, ev = abuf(1024, BF16)
                ev = ev[:, 0:nq]
                P.act(ev, sp_, AF.Exp, scale=ATT_SCALE)
                ps_free(ph)
                pend[i] = (he, ev)

            acc = {}

            def emit_PV(i):
                h_, pi, ki = units[i]
                q0, nq, kcs = probs[pi]
                kc = kcs[ki]
                x = hd[h_]
                he, ev = pend.pop(i)
                if ki == 0:
                    acc[(h_, pi)] = (ps_alloc(1), ps_alloc(1))
                po, pl = acc[(h_, pi)]
                last = (ki == len(kcs) - 1)
                P.mm(PSb(po, 0, n=nq), x['v3'][:, kc, :], ev, start=(ki == 0), stop=last)
                P.mm(PSb(pl, 0, n=nq), ones, ev, start=(ki == 0), stop=last)
                A.free(he)
                if last:
                    hrl, rl = abuf(2048, F32)
                    rl = rl[:, 0:nq]
                    P.op('dve', (lambda e, o=rl.ap, i_=PSb(pl, 0, n=nq).ap: e.reciprocal(o, i_)),
                         [PSb(pl, 0, n=nq)], [rl])
                    P.tt('dve', ao[h_][1][:, q0:q0 + nq], PSb(po, 0, n=nq), rl, ALU.mult)
                    A.free(hrl)
                    ps_free(po); ps_free(pl)
                    del acc[(h_, pi)]
                    if pi == len(probs) - 1:
                        for hh in x['hs']:
                            A.free(hh)
                        del hd[h_]

            LA = cfg.get('la_s', 4) if S else cfg.get('la_p', 6)
            for i in range(len(units) + LA):
                if i < len(units):
                    emit_S(i)
                if i >= LA:
                    emit_PV(i - LA)
            if S:
                A.free(hcc); A.free(hss)
            A.free(hck); A.free(hk16)
            for m in range(4):
                A.free(qlat[m][0])
            wo = dr["w_mla_o"][j].rearrange("(k p) n -> p k n", p=128)
            proj_residual(lambda m: wo[:, :, m * 128:(m + 1) * 128], NK, [a_[1] for a_ in ao],
                          lambda m: modcol(l, c, 2, m), stats=cfg.get("ffn", True))
            for a_ in ao:
                A.free(a_[0])

        for pname in passes:
            S = (pname == "S")
            c = 1 if S else 0
            load_x(dr["xs"] if S else dr["xp"])
            for l in cfg.get("layers", range(nlayers)):
                ada_flush(l)
                if cfg.get("mixer", True):
                    if l % 2 == 0:
                        mla(l, c, S)
                    else:
                        rec(l, c, S)
                if cfg.get("ffn", True):
                    ffn(l, c, 1 if S else 4)
            final_norm(dr["ys"] if S else dr["yp"])
        assert not any(A.used), "arena leak"
        assert not any(ps_used), "psum leak"
        P.emit_all(block, esem, dsem)
    return nc, P


def _rope_tables():
    rows = T // 64
    row = np.repeat(np.arange(rows), 64).astype(np.float32)
    col = np.tile(np.arange(64), rows).astype(np.float32)
    inv = (1.0 / (np.float32(10000.0) ** (np.arange(0, 32, 2, dtype=np.float32) / np.float32(32)))).astype(np.float32)
    ar = (row[:, None] * inv).astype(np.float32)
    ac = (col[:, None] * inv).astype(np.float32)
    cr, sr, cc, sc = np.cos(ar).T, np.sin(ar).T, np.cos(ac).T, np.sin(ac).T
    C = np.concatenate([cr, cr, cc, cc], axis=0).astype(np.float32)
    S_ = np.concatenate([-sr, sr, -sc, sc], axis=0).astype(np.float32)
    return np.ascontiguousarray(C), np.ascontiguousarray(S_)


_CACHE = {}


def make_in_maps(inputs, cores):
    C, S_ = _rope_tables()
    ident = np.eye(128, dtype=np.float32)
    f = lambda a: np.ascontiguousarray(np.asarray(a, dtype=np.float32))
    shared = {k_: f(inputs[k_]) for k_ in W_SHAPES}
    maps = []
    for i in cores:
        m = dict(shared)
        m["xp"] = f(inputs["x_prompt"][4 * i:4 * i + 4]).reshape(1024, 2048)
        m["xs"] = f(inputs["x_sample"][i])
        m["cckv"] = f(inputs["cache_ckv"][i])
        m["ckr"] = f(inputs["cache_krope"][i])
        m["slru"] = f(inputs["state_lru"][i])
        m["cvec"] = np.stack([f(inputs["c_ctx"]), f(inputs["c"][i])], axis=0)
        m["ident"] = ident
        m["ropec"] = C
        m["ropes"] = S_
        maps.append(m)
    return maps


def kernel(**inputs):
    n = 8
    if "nc" not in _CACHE:
        _CACHE["nc"] = build_program()[0]
    nc = _CACHE["nc"]
    maps = make_in_maps(inputs, list(range(n)))
    res = run_bass_kernel_spmd(nc, maps, core_ids=list(range(n)))
    r = res.results
    y_prompt = np.concatenate([r[i]["yp"].reshape(4, 256, 2048) for i in range(n)], axis=0)
    y_sample = np.stack([r[i]["ys"] for i in range(n)], axis=0)
    ockv = np.concatenate([r[i]["ockv"] for i in range(n)], axis=0)
    okr = np.concatenate([r[i]["okr"] for i in range(n)], axis=0)
    olru = np.concatenate([r[i]["olru"] for i in range(n)], axis=0)
    return (y_prompt.astype(np.float32), y_sample.astype(np.float32), ockv.astype(np.float32),
            okr.astype(np.float32), olru.astype(np.float32))
```

```python
import numpy as np
import concourse.bass as bass
import concourse.mybir as mybir

F32 = mybir.dt.float32
BF16 = mybir.dt.bfloat16
AF = mybir.ActivationFunctionType
ALU = mybir.AluOpType

ENG_NAMES = ["pe", "act", "dve", "pool", "sp"]


class Tile:
    __slots__ = ("w", "r")

    def __init__(self):
        self.w = None
        self.r = {}


class V:
    __slots__ = ("ap", "tiles")

    def __init__(self, ap, tiles):
        self.ap = ap
        self.tiles = tuple(tiles)

    def __getitem__(self, idx):
        return V(self.ap[idx], self.tiles)

    def bitcast(self, dt):
        return V(self.ap.bitcast(dt), self.tiles)

    def rearrange(self, pattern, **kw):
        return V(self.ap.rearrange(pattern, **kw), self.tiles)

    def sub(self, tiles):
        return V(self.ap, tiles)


def _ap(x):
    return x.ap if isinstance(x, V) else x


class Prog:
    def __init__(self, nc, n_dma_sems=32):
        self.nc = nc
        self.ops = {e: [] for e in ENG_NAMES}
        self.NS = n_dma_sems
        self.n_dma = 0
        self.n_dma_q = {}
        self.dma_uid = 0

    def _collect(self, reads, writes, me, rkey):
        deps = []
        for v in reads:
            for t in v.tiles:
                if t.w is not None:
                    deps.append(t.w)
        for v in writes:
            for t in v.tiles:
                if t.w is not None:
                    deps.append(t.w)
                deps.extend(t.r.values())
        for v in reads:
            for t in v.tiles:
                t.r[rkey] = me
        for v in writes:
            for t in v.tiles:
                t.w = me
                t.r = {}
        return [d for d in deps if d != me]

    def _mark(self, eng, deps):
        for d in deps:
            if d[0] == 'e':
                if d[1] == 'pe' and eng == 'pe':
                    continue
                self.ops[d[1]][d[2]]['signal'] = True

    def op(self, eng, emit, reads, writes):
        reads = [r for r in reads if isinstance(r, V)]
        writes = [w for w in writes if isinstance(w, V)]
        idx = len(self.ops[eng])
        me = ('e', eng, idx)
        deps = self._collect(reads, writes, me, eng)
        self.ops[eng].append(dict(emit=emit, deps=deps, signal=False))
        self._mark(eng, deps)

    def dma(self, q, out, in_, **kw):
        n = self.n_dma_q.get(q, 0)
        self.n_dma_q[q] = n + 1
        self.n_dma += 1
        s = n % self.NS
        val = 16 * (n // self.NS + 1)
        me = ('d', (q, s), val)
        reads = [in_] if isinstance(in_, V) else []
        writes = [out] if isinstance(out, V) else []
        self.dma_uid += 1
        deps = self._collect(reads, writes, me, ('d', self.dma_uid))
        if val > 16:
            deps.append(('d', (q, s), val - 16))
        o, i = _ap(out), _ap(in_)
        self.ops[q].append(dict(emit=lambda e: e.dma_start(out=o, in_=i, **kw), deps=deps,
                                signal=False, dma=((q, s), val)))
        self._mark(q, deps)

    def mm(self, out, lhsT, rhs, start=True, stop=True):
        o, l, r = _ap(out), _ap(lhsT), _ap(rhs)
        self.op('pe', lambda e: e.matmul(o, l, r, start=start, stop=stop), [lhsT, rhs], [out])

    def transpose(self, out, in_, ident):
        o, i, d = _ap(out), _ap(in_), _ap(ident)
        self.op('pe', lambda e: e.transpose(o, i, d), [in_, ident], [out])

    def act(self, out, in_, func, bias=None, scale=1.0, accum_out=None):
        o, i = _ap(out), _ap(in_)
        kw = {}
        if bias is not None:
            kw['bias'] = _ap(bias)
        kw['scale'] = _ap(scale)
        if accum_out is not None:
            kw['accum_out'] = _ap(accum_out)
        self.op('act', lambda e: e.activation(o, i, func, **kw), [in_, bias, scale],
                [out, accum_out])

    def tt(self, eng, out, in0, in1, op):
        o, a, b = _ap(out), _ap(in0), _ap(in1)
        self.op(eng, lambda e: e.tensor_tensor(o, a, b, op), [in0, in1], [out])

    def ts(self, eng, out, in0, s1, op0, s2=None, op1=None):
        o, a, x1, x2 = _ap(out), _ap(in0), _ap(s1), _ap(s2)
        if op1 is None:
            self.op(eng, lambda e: e.tensor_scalar(o, a, x1, None, op0), [in0, s1], [out])
        else:
            self.op(eng, lambda e: e.tensor_scalar(o, a, x1, x2, op0, op1), [in0, s1, s2], [out])

    def stt(self, eng, out, in0, scalar, in1, op0, op1):
        o, a, s, b = _ap(out), _ap(in0), _ap(scalar), _ap(in1)
        self.op(eng, lambda e: e.scalar_tensor_tensor(o, a, s, b, op0, op1), [in0, scalar, in1], [out])

    def copy(self, eng, out, in_):
        o, i = _ap(out), _ap(in_)
        if eng == 'act':
            self.op(eng, lambda e: e.copy(o, i), [in_], [out])
        else:
            self.op(eng, lambda e: e.tensor_copy(o, i), [in_], [out])

    def memset(self, eng, out, val):
        o = _ap(out)
        self.op(eng, lambda e: e.memset(o, val), [], [out])

    def scan(self, out, d0, d1, init, op0=ALU.mult, op1=ALU.add, eng='dve'):
        o, a, b, i = _ap(out), _ap(d0), _ap(d1), _ap(init)
        self.op(eng, lambda e: e.tensor_tensor_scan(o, a, b, i, op0, op1), [d0, d1, init], [out])

    def emit_all(self, block, esem, dsem):
        nc = self.nc
        final = []
        for q, nq in self.n_dma_q.items():
            for s in range(min(self.NS, nq)):
                cnt = (nq - 1 - s) // self.NS + 1
                final.append(('d', (q, s), 16 * cnt))
        self.ops['sp'].append(dict(emit=None, deps=final, signal=False))
        for e in ENG_NAMES:
            c = 0
            for r in self.ops[e]:
                if r['signal'] and 'dma' not in r:
                    c += 1
                    r['tick'] = c
        self.nwaits = {e: 0 for e in ENG_NAMES}

        def run(ename, eh):
            seen = {}
            for r in self.ops[ename]:
                need = {}
                for d in r['deps']:
                    if d[0] == 'e':
                        if d[1] == 'pe' and ename == 'pe':
                            continue
                        key = ('e', d[1])
                        val = self.ops[d[1]][d[2]]['tick']
                    else:
                        key = ('d', d[1])
                        val = d[2]
                    if seen.get(key, 0) >= val:
                        continue
                    if need.get(key, 0) < val:
                        need[key] = val
                for key, val in need.items():
                    sem = esem[key[1]] if key[0] == 'e' else dsem[key[1][0]][key[1][1]]
                    eh.wait_ge(sem, val)
                    seen[key] = val
                    self.nwaits[ename] += 1
                if r['emit'] is None:
                    continue
                ins = r['emit'](eh)
                if 'dma' in r:
                    ins.then_inc(dsem[r['dma'][0][0]][r['dma'][0][1]], 16)
                elif r['signal']:
                    ins.then_inc(esem[ename], 1)

        @block.tensor
        def _(e):
            run('pe', e)

        @block.scalar
        def _(e):
            run('act', e)

        @block.vector
        def _(e):
            run('dve', e)

        @block.gpsimd
        def _(e):
            run('pool', e)

        @block.sync
        def _(e):
            run('sp', e)


class Arena:
    PAGEW = 512

    def __init__(self, t, npages):
        self.t = t
        self.np = npages
        self.tiles = [Tile() for _ in range(npages)]
        self.used = [False] * npages
        self.ptr = 0

    def alloc(self, nbytes, top=False):
        n = (nbytes + 4 * self.PAGEW - 1) // (4 * self.PAGEW)
        if top:
            for s in range(self.np - n, -1, -1):
                if not any(self.used[s:s + n]):
                    for i in range(s, s + n):
                        self.used[i] = True
                    return (s, n)
            raise RuntimeError(f"arena full(top): need {n} pages, used {sum(self.used)}/{self.np}")
        start = self.ptr
        for off in range(self.np):
            s = (start + off) % self.np
            if s + n > self.np:
                continue
            if not any(self.used[s:s + n]):
                for i in range(s, s + n):
                    self.used[i] = True
                self.ptr = (s + n) % self.np
                return (s, n)
        raise RuntimeError(f"arena full: need {n} pages, used {sum(self.used)}/{self.np}")

    def free(self, h):
        s, n = h
        for i in range(s, s + n):
            assert self.used[i]
            self.used[i] = False

    def view(self, h, dt, shape=None, parts=128):
        s, n = h
        ap = self.t[0:parts, s * self.PAGEW:(s + n) * self.PAGEW]
        if dt != F32:
            ap = ap.bitcast(dt)
        v = V(ap, self.tiles[s:s + n])
        return v

import contextlib
from concourse.bass_utils import run_bass_kernel_spmd

D = 2048
T = 1024
NK = 16
DFF = 5632
NF = 44
EPS = 1e-6
ATT_SCALE = float(192 ** -0.5)
NPAGES = 62

W_SHAPES = dict(
    g_mix=(4, 2048), g_ffn=(4, 2048), g_final=(2048,), w_ada=(4, 2048, 12288), b_ada=(4, 12288),
    w_mla_in=(2, 2048, 1088), g_mla_q=(2, 512), g_mla_kv=(2, 512), w_mla_uq=(2, 512, 3072),
    w_mla_uk=(2, 512, 2048), w_mla_uv=(2, 512, 2048), w_mla_o=(2, 2048, 2048),
    w_rec_in=(2, 2048, 4096), w_rec_conv=(2, 4, 2048), b_rec_conv=(2, 2048),
    w_rec_gx=(2, 2, 16, 128, 128), b_rec_gx=(2, 2, 2048), w_rec_ga=(2, 2, 16, 128, 128),
    b_rec_ga=(2, 2, 2048), rec_lambda=(2, 2, 2048), w_rec_out=(2, 2048, 2048),
    w_ffn_up=(4, 2048, 11264), w_ffn_conv=(4, 3, 11264), b_ffn_conv=(4, 11264),
    w_ffn_down=(4, 5632, 2048),
)


def build_program(cfg=None):
    cfg = cfg or {}
    passes = cfg.get("passes", "PS")
    nlayers = cfg.get("nlayers", 4)
    nc = bass.Bass("TRN2", target_bir_lowering=False)
    dr = {}

    def din(name, shape):
        dr[name] = nc.dram_tensor(name, list(shape), F32, kind="ExternalInput").ap()

    din("xp", (1024, 2048)); din("xs", (1024, 2048)); din("cckv", (2, 256, 512)); din("ckr", (2, 256, 64))
    din("slru", (2, 2, 2048)); din("cvec", (2, 2048))
    for k_, s_ in W_SHAPES.items():
        din(k_, s_)
    din("ident", (128, 128)); din("ropec", (64, 1024)); din("ropes", (64, 1024))

    def dout(name, shape):
        dr[name] = nc.dram_tensor(name, list(shape), F32, kind="ExternalOutput").ap()

    dout("yp", (1024, 2048)); dout("ys", (1024, 2048)); dout("ockv", (4, 2, 256, 512))
    dout("okr", (4, 2, 256, 64)); dout("olru", (4, 2, 2, 2048))

    vec_srcs = [
        ("cvec", dr["cvec"].rearrange("c (k p) -> (c k) p", p=128), 32),
        ("g_mix", dr["g_mix"].rearrange("l (k p) -> (l k) p", p=128), 64),
        ("g_ffn", dr["g_ffn"].rearrange("l (k p) -> (l k) p", p=128), 64),
        ("g_final", dr["g_final"].rearrange("(k p) -> k p", p=128), 16),
        ("b_ada", dr["b_ada"].rearrange("l (k p) -> (l k) p", p=128), 384),
        ("g_mla_q", dr["g_mla_q"].rearrange("l (k p) -> (l k) p", p=128), 8),
        ("g_mla_kv", dr["g_mla_kv"].rearrange("l (k p) -> (l k) p", p=128), 8),
        ("w_rec_conv", dr["w_rec_conv"].rearrange("j t (k p) -> (j t k) p", p=128), 128),
        ("b_rec_conv", dr["b_rec_conv"].rearrange("j (k p) -> (j k) p", p=128), 32),
        ("b_rec_gx", dr["b_rec_gx"].rearrange("j d (k p) -> (j d k) p", p=128), 64),
        ("b_rec_ga", dr["b_rec_ga"].rearrange("j d (k p) -> (j d k) p", p=128), 64),
        ("rec_lambda", dr["rec_lambda"].rearrange("j d (k p) -> (j d k) p", p=128), 64),
        ("w_ffn_conv", dr["w_ffn_conv"].rearrange("l t (k p) -> (l t k) p", p=128), 1056),
        ("b_ffn_conv", dr["b_ffn_conv"].rearrange("l (k p) -> (l k) p", p=128), 352),
        ("slru", dr["slru"].rearrange("j d (k p) -> (j d k) p", p=128), 64),
    ]
    voff = {}
    o_ = 0
    for name, _, r_ in vec_srcs:
        voff[name] = o_
        o_ += r_
    NVEC = o_
    der = {}
    for name, n_ in [("mod", 768), ("a1", 128), ("a2", 128), ("cph", 64), ("cp", 64), ("hbgx", 64),
                     ("hbga", 64), ("one", 1), ("st", 256)]:
        der[name] = o_
        o_ += n_
    NCST = o_

    with contextlib.ExitStack() as es:
        xb_t = es.enter_context(nc.sbuf_tensor("xb", [128, NK, T], F32))
        ar_t = es.enter_context(nc.sbuf_tensor("arena", [128, NPAGES * 512], F32))
        cst_t = es.enter_context(nc.sbuf_tensor("cst", [128, NCST], F32))
        id_t = es.enter_context(nc.sbuf_tensor("identf", [128, 128], F32))
        ones_t = es.enter_context(nc.sbuf_tensor("onesb", [128, 128], BF16))
        sc_t = es.enter_context(nc.sbuf_tensor("scb", [128, 32], BF16))
        ps_t = es.enter_context(nc.psum_tensor("ps", [128, 8 * 512], F32))
        esem = {e: es.enter_context(nc.semaphore("e_" + e)) for e in ENG_NAMES}
        NDS = 24
        dsem = {q: [es.enter_context(nc.semaphore(f"d{q}{i}")) for i in range(NDS)] for q in ("sp", "pool")}
        block = es.enter_context(nc.Block())

        P = Prog(nc, n_dma_sems=NDS)
        A = Arena(ar_t, NPAGES)
        xtiles = [Tile() for _ in range(NK)]
        pstiles = [Tile() for _ in range(8)]
        ps_used = [False] * 8
        ps_ptr = [0]

        def XB(k0, k1=None, t0=0, t1=T):
            if k1 is None:
                return V(xb_t[:, k0, t0:t1], [xtiles[k0]])
            return V(xb_t[:, k0:k1, t0:t1], xtiles[k0:k1])

        def ps_alloc(n=1):
            start = ps_ptr[0]
            for off in range(8):
                s = (start + off) % 8
                if n == 2 and s % 2:
                    continue
                if s + n > 8 or any(ps_used[s:s + n]):
                    continue
                for i in range(s, s + n):
                    ps_used[i] = True
                ps_ptr[0] = (s + n) % 8
                return (s, n)
            raise RuntimeError("psum full")

        def ps_free(h):
            for i in range(h[0], h[0] + h[1]):
                assert ps_used[i]
                ps_used[i] = False

        def PSv(h, parts=128):
            s, n = h
            return V(ps_t[0:parts, s * 512:(s + n) * 512], pstiles[s:s + n])

        def PSb(h, i, parts=128, n=512):
            s = h[0] + i
            return V(ps_t[0:parts, s * 512:s * 512 + n], [pstiles[s]])

        vt = {name: Tile() for name in list(voff) + list(der)}

        per_layer = {"mod": 192, "a1": 32, "a2": 32}
        vtl = {nm: [Tile() for _ in range(4)] for nm in per_layer}

        def vcol(name, i, n=1):
            base = voff[name] if name in voff else der[name]
            if name in per_layer:
                return V(cst_t[:, base + i:base + i + n], [vtl[name][i // per_layer[name]]])
            return V(cst_t[:, base + i:base + i + n], [vt[name]])

        ident = V(id_t[:, :], [Tile()])
        ones = V(ones_t[:, :], [Tile()])
        scb = V(sc_t[:, :], [Tile()])

        def abuf(nbytes, dt, parts=128, top=False):
            h = A.alloc(nbytes, top=top)
            return h, A.view(h, dt, parts=parts)

        evac_rr = [0]
        EW2 = cfg.get("ew2", "dve")

        P.dma('sp', ident, dr["ident"])
        P.memset('dve', ones, 1.0)
        P.memset('dve', vcol("one", 0), 1.0)
        for name, src, R in vec_srcs:
            for r0 in range(0, R, 128):
                rr = min(128, R - r0)
                h, stg = abuf(512, F32)
                stg_r = V(stg.ap[0:rr, 0:128], stg.tiles)
                P.dma('sp', stg_r, src[r0:r0 + rr, :])
                ph = ps_alloc(1)
                pv = PSb(ph, 0, n=rr)
                P.transpose(pv, stg_r, V(ident.ap[0:rr, 0:rr], ident.tiles))
                P.copy('dve', vcol(name, r0, rr), pv)
                ps_free(ph)
                A.free(h)
        P.act(scb, vcol("cvec", 0, 32), AF.Silu)
        h, tmpv = abuf(512, F32)
        e1 = tmpv[:, 0:64]
        P.act(e1, vcol("rec_lambda", 0, 64), AF.Exp, scale=-1.0)
        P.ts('dve', e1, e1, 1.0, ALU.add)
        P.act(e1, e1, AF.Ln)
        P.ts('dve', vcol("cph", 0, 64), e1, -4.0, ALU.mult)
        P.ts('dve', vcol("cp", 0, 64), e1, -8.0, ALU.mult)
        A.free(h)
        P.ts('dve', vcol("hbgx", 0, 64), vcol("b_rec_gx", 0, 64), 0.5, ALU.mult)
        P.ts('dve', vcol("hbga", 0, 64), vcol("b_rec_ga", 0, 64), 0.5, ALU.mult)
        scv = scb.rearrange("p (c k) -> p k c", c=2)
        ada_pending = {}

        ada_todo = [(l, jb) for l in range(4) for jb in range(96)]
        ada_ready = []
        ada_s2 = []

        def ada_prep():
            if not ada_todo:
                return
            l_, jb = ada_todo.pop(0)
            h, wv = abuf(16 * 128 * 2, BF16)
            w3 = wv.rearrange("p (k c) -> p k c", k=16)
            P.dma('pool', w3, dr["w_ada"][l_].rearrange("(k p) n -> p k n", p=128)[:, :, jb * 128:(jb + 1) * 128])
            ada_ready.append((l_, jb, h, w3))

        def ada_stage2():
            items, h2, row = ada_s2.pop(0)
            ph2 = ps_alloc(1)
            for i in range(len(items)):
                P.transpose(PSb(ph2, 0)[:, 2 * i:2 * i + 2], row[:, i * 128:(i + 1) * 128],
                            V(ident.ap[0:2, 0:2], ident.tiles))
            A.free(h2)
            i = 0
            while i < len(items):
                l, jb0 = items[i]
                cnt = 1
                while i + cnt < len(items) and items[i + cnt] == (l, jb0 + cnt):
                    cnt += 1
                src = PSb(ph2, 0)[:, 2 * i:2 * (i + cnt)].rearrange("p (j c) -> p j c", c=2)
                for c in range(2):
                    P.copy('dve', vcol("mod", (l * 2 + c) * 96 + jb0, cnt), src[:, :, c])
                if jb0 + cnt == 96:
                    for c in range(2):
                        P.tt('dve', vcol("mod", (l * 2 + c) * 96, 96), vcol("mod", (l * 2 + c) * 96, 96),
                             vcol("b_ada", l * 96, 96), ALU.add)
                        mb = (l * 2 + c) * 96
                        P.stt('dve', vcol("a1", (l * 2 + c) * 16, 16), vcol("mod", mb + 16, 16), 1.0,
                              vcol("g_mix", l * 16, 16), ALU.add, ALU.mult)
                        P.stt('dve', vcol("a2", (l * 2 + c) * 16, 16), vcol("mod", mb + 64, 16), 1.0,
                              vcol("g_ffn", l * 16, 16), ALU.add, ALU.mult)
                i += cnt
            ps_free(ph2)

        def ada_stage1():
            while ada_ready:
                batch = ada_ready[:4]
                del ada_ready[:4]
                ph = ps_alloc(1)
                for i, (l, jb, h, w3) in enumerate(batch):
                    for k in range(NK):
                        P.mm(PSb(ph, 0, parts=2)[:, i * 128:(i + 1) * 128], scv[:, k, :], w3[:, k, :],
                             start=(k == 0), stop=(k == NK - 1))
                    A.free(h)
                h2, row = abuf(2048, F32, parts=2)
                n_ = len(batch) * 128
                P.copy('act', row[:, 0:n_], PSb(ph, 0, parts=2, n=n_))
                ps_free(ph)
                ada_s2.append(([(b[0], b[1]) for b in batch], h2, row))

        def ada_tick(n_items):
            if not (ada_todo or ada_ready or ada_s2):
                return
            while ada_s2:
                ada_stage2()
            ada_stage1()
            for _ in range(n_items):
                ada_prep()

        def ada_flush(upto):
            while ada_todo and ada_todo[0][0] <= upto:
                ada_tick(4)
            if any(r[0] <= upto for r in ada_ready) or any(it[0][0][0] <= upto for it in ada_s2):
                ada_tick(0)
                ada_tick(0)

        def modcol(l, c, which, k):
            return vcol("mod", (l * 2 + c) * 96 + which * 16 + k)

        def load_x(src):
            for tb in range(8):
                h, stg = abuf(8192, F32)
                P.dma('sp', stg, src[tb * 128:(tb + 1) * 128, :])
                for kg in range(4):
                    ph = ps_alloc(1)
                    for kk in range(4):
                        k = kg * 4 + kk
                        P.transpose(PSb(ph, 0)[:, kk * 128:(kk + 1) * 128], stg[:, k * 128:(k + 1) * 128], ident)
                    dst = XB(kg * 4, kg * 4 + 4, tb * 128, (tb + 1) * 128)
                    srcv = PSb(ph, 0).rearrange("p (a b) -> p a b", a=4)
                    P.copy('act' if kg % 2 else 'dve', dst, srcv)
                    ps_free(ph)
                A.free(h)

        def rstd_of(chunks, dim):
            hr, rs = abuf(4096, F32)
            for half in range(2):
                ph = ps_alloc(1)
                n = len(chunks)
                for k, ch in enumerate(chunks):
                    hs, sq = abuf(1024, BF16)
                    sq = sq[:, 0:512]
                    P.act(sq, ch[:, half * 512:(half + 1) * 512], AF.Square)
                    P.mm(PSb(ph, 0), ones, sq, start=(k == 0), stop=(k == n - 1))
                    A.free(hs)
                rsh = rs[:, half * 512:(half + 1) * 512]
                P.ts('dve', rsh, PSb(ph, 0), 1.0 / dim, ALU.mult, EPS, ALU.add)
                ps_free(ph)
                P.act(rsh, rsh, AF.Sqrt)
                P.op('dve', (lambda e, o=rsh.ap: e.reciprocal(o, o)), [rsh], [rsh])
            return hr, rs

        def norm_mod(acol, shcol):
            hr, rs = rstd_of([XB(k) for k in range(NK)], D)
            hb = []
            for k in range(NK):
                ht, tmp = abuf(4096, F32)
                P.tt('dve', tmp, XB(k), rs, ALU.mult)
                hh, hv = abuf(2048, BF16, top=True)
                P.act(hv, tmp, AF.Identity, bias=shcol(k), scale=acol(k))
                A.free(ht)
                hb.append((hh, hv))
            A.free(hr)
            return hb

        def proj_residual(w_src_fn, nk_in, rhs_chunks, gcol):
            def load(m):
                hw, wv = abuf(nk_in * 256, BF16)
                w3 = wv[:, 0:nk_in * 128].rearrange("p (k c) -> p k c", k=nk_in)
                P.dma('pool', w3, w_src_fn(m))
                return hw, w3
            nxt = load(0)
            for m in range(NK):
                hw, w3 = nxt
                if m + 1 < NK:
                    nxt = load(m + 1)
                ph = ps_alloc(2)
                for k in range(nk_in):
                    for half in range(2):
                        P.mm(PSb(ph, half), w3[:, k, :], rhs_chunks[k][:, half * 512:(half + 1) * 512],
                             start=(k == 0), stop=(k == nk_in - 1))
                A.free(hw)
                P.stt('dve', XB(m), PSv(ph), gcol(m), XB(m), ALU.mult, ALU.add)
                ps_free(ph)

        def ffn(l, c, segs):
            hb = norm_mod(lambda k: vcol("a2", (l * 2 + c) * 16 + k), lambda k: modcol(l, c, 3, k))
            L = T // segs
            wup = dr["w_ffn_up"][l].rearrange("(k p) n -> p k n", p=128)
            wdn = dr["w_ffn_down"][l].rearrange("(f p) n -> p f n", p=128)

            def load_up(f):
                hw, wv = abuf(16 * 256 * 2, BF16)
                w3 = wv.rearrange("p (k c) -> p k c", k=16)
                P.dma('pool', w3[:, :, 0:128], wup[:, :, f * 128:(f + 1) * 128])
                P.dma('pool', w3[:, :, 128:256], wup[:, :, DFF + f * 128: DFF + (f + 1) * 128])
                return hw, w3
            nxt = load_up(0)
            for (f0, nf) in [(0, 11), (11, 11), (22, 11), (33, 11)]:
                ab = []
                for f in range(f0, f0 + nf):
                    hw, w3 = nxt
                    if f + 1 < NF:
                        nxt = load_up(f + 1)
                    us = []
                    for part in range(2):
                        ch = f + part * NF
                        ph = ps_alloc(2)
                        for k in range(NK):
                            for half in range(2):
                                P.mm(PSb(ph, half), w3[:, k, part * 128:(part + 1) * 128],
                                     hb[k][1][:, half * 512:(half + 1) * 512], start=(k == 0), stop=(k == NK - 1))
                        hu, u = abuf(4096, F32)
                        pv = PSv(ph)
                        P.act(u, pv, AF.Identity, bias=vcol("b_ffn_conv", l * 88 + ch),
                              scale=vcol("w_ffn_conv", (l * 3 + 1) * 88 + ch))
                        u3 = u.rearrange("p (s t) -> p s t", s=segs)
                        p3 = pv.rearrange("p (s t) -> p s t", s=segs)
                        P.stt('dve', u3[:, :, 1:], p3[:, :, :-1], vcol("w_ffn_conv", (l * 3 + 0) * 88 + ch),
                              u3[:, :, 1:], ALU.mult, ALU.add)
                        P.stt('dve', u3[:, :, :-1], p3[:, :, 1:], vcol("w_ffn_conv", (l * 3 + 2) * 88 + ch),
                              u3[:, :, :-1], ALU.mult, ALU.add)
                        ps_free(ph)
                        us.append((hu, u))
                    A.free(hw)
                    hg, sg = abuf(4096, F32)
                    P.act(sg, us[0][1], AF.Silu)
                    ha, av = abuf(2048, BF16, top=True)
                    P.tt('dve', av, sg, us[1][1], ALU.mult)
                    A.free(hg); A.free(us[0][0]); A.free(us[1][0])
                    ab.append((ha, av))
                    ada_tick(1 if (ada_todo and ada_todo[0][0] <= l + 1) else 0)
                proj_residual(lambda m: wdn[:, f0:f0 + nf, m * 128:(m + 1) * 128], nf, [a[1] for a in ab],
                              lambda m: modcol(l, c, 5, m))
                for a in ab:
                    A.free(a[0])
            for hh, _ in hb:
                A.free(hh)

        def final_norm(dst):
            hr, rs = rstd_of([XB(k) for k in range(NK)], D)
            for k in range(NK):
                P.stt('dve', XB(k), XB(k), vcol("g_final", k), rs, ALU.mult, ALU.mult)
            A.free(hr)
            for tb in range(8):
                h, stg = abuf(8192, F32)
                for kg in range(4):
                    ph = ps_alloc(1)
                    for kk in range(4):
                        k = kg * 4 + kk
                        P.transpose(PSb(ph, 0)[:, kk * 128:(kk + 1) * 128], XB(k, None, tb * 128, (tb + 1) * 128), ident)
                    P.copy('act' if kg % 2 else 'dve', stg[:, kg * 512:(kg + 1) * 512], PSb(ph, 0))
                    ps_free(ph)
                P.dma('sp', dst[tb * 128:(tb + 1) * 128, :], stg)
                A.free(h)

        def rec(l, c, S):
            j = l // 2
            segs = 1 if S else 4
            L = T // segs
            hb = norm_mod(lambda k: vcol("a1", (l * 2 + c) * 16 + k), lambda k: modcol(l, c, 0, k))
            win = dr["w_rec_in"][j].rearrange("(k p) n -> p k n", p=128)
            mixed = []
            stA = {}

            def stage_A(n):
                hw, wv = abuf(16 * 256 * 2, BF16)
                w3 = wv.rearrange("p (k c) -> p k c", k=16)
                P.dma('pool', w3[:, :, 0:128], win[:, :, n * 128:(n + 1) * 128])
                P.dma('pool', w3[:, :, 128:256], win[:, :, D + n * 128: D + (n + 1) * 128])
                hgw, gwv = abuf(4 * 128 * 2, BF16)
                gw = gwv[:, 0:512].rearrange("p (a o) -> p a o", a=4)
                for d in range(2):
                    P.dma('pool', gw[:, d * 2 + 0, :], dr["w_rec_gx"][j, d, n])
                    P.dma('pool', gw[:, d * 2 + 1, :], dr["w_rec_ga"][j, d, n])
                phy = ps_alloc(2)
                phx = ps_alloc(2)
                for part, ph in ((0, phy), (1, phx)):
                    for k in range(NK):
                        for half in range(2):
                            P.mm(PSb(ph, half), w3[:, k, part * 128:(part + 1) * 128],
                                 hb[k][1][:, half * 512:(half + 1) * 512], start=(k == 0), stop=(k == NK - 1))
                A.free(hw)
                hy, y32 = abuf(4096, F32)
                P.copy('act', y32, PSv(phy))
                ps_free(phy)
                ht1, t1 = abuf(4096, F32)
                P.tt(EW2, t1, y32, y32, ALU.mult)
                P.ts(EW2, t1, t1, 0.044715, ALU.mult, 1.0, ALU.add)
                P.tt(EW2, t1, t1, y32, ALU.mult)
                P.act(t1, t1, AF.Tanh, scale=0.7978845608028654)
                P.stt('dve', t1, t1, 1.0, y32, ALU.add, ALU.mult)
                A.free(hy)
                hxc, xc = abuf(4096, F32)
                pv = PSv(phx)
                cw = lambda t: vcol("w_rec_conv", (j * 4 + t) * 16 + n)
                P.act(xc, pv, AF.Identity, bias=vcol("b_rec_conv", j * 16 + n), scale=cw(2))
                x3 = xc.rearrange("p (s t) -> p s t", s=segs)
                p3 = pv.rearrange("p (s t) -> p s t", s=segs)
                P.stt('dve', x3[:, :, 2:], p3[:, :, :-2], cw(0), x3[:, :, 2:], ALU.mult, ALU.add)
                P.stt('dve', x3[:, :, 1:], p3[:, :, :-1], cw(1), x3[:, :, 1:], ALU.mult, ALU.add)
                P.stt('dve', x3[:, :, :-1], p3[:, :, 1:], cw(3), x3[:, :, :-1], ALU.mult, ALU.add)
                ps_free(phx)
                hx16, xc16 = abuf(2048, BF16)
                P.copy('act', xc16, xc)
                stA[n] = dict(ht1=ht1, t1=t1, hxc=hxc, xc=xc, hx16=hx16, xc16=xc16, hgw=hgw, gw=gw)

            def stage_B(n):
                sa = stA.pop(n)
                xc, xc16, gw, t1 = sa['xc'], sa['xc16'], sa['gw'], sa['t1']
                dd = []
                for d in range(2):
                    cidx = (j * 2 + d) * 16 + n
                    pgx = ps_alloc(2)
                    pga = ps_alloc(2)
                    for half in range(2):
                        P.mm(PSb(pgx, half), gw[:, d * 2 + 0, :], xc16[:, half * 512:(half + 1) * 512])
                        P.mm(PSb(pga, half), gw[:, d * 2 + 1, :], xc16[:, half * 512:(half + 1) * 512])
                    hgxt, gx = abuf(4096, F32)
                    hgat, ga = abuf(4096, F32)
                    P.act(gx, PSv(pgx), AF.Tanh, bias=vcol("hbgx", cidx), scale=0.5)
                    P.act(ga, PSv(pga), AF.Tanh, bias=vcol("hbga", cidx), scale=0.5)
                    ps_free(pgx); ps_free(pga)
                    hat, a = abuf(4096, F32)
                    P.act(a, ga, AF.Exp, bias=vcol("cph", cidx), scale=vcol("cph", cidx))
                    P.act(ga, ga, AF.Exp, bias=vcol("cp", cidx), scale=vcol("cp", cidx))
                    P.stt('dve', gx, gx, 1.0, xc, ALU.add, ALU.mult)
                    dd.append(dict(gx=gx, hgx=hgxt, ga=ga, hga=hgat, a=a, ha=hat, cidx=cidx))
                A.free(sa['hx16']); A.free(sa['hxc']); A.free(sa['hgw'])
                for x_ in dd:
                    P.act(x_['ga'], x_['ga'], AF.Sqrt, bias=vcol("one", 0), scale=-1.0)
                hs = []
                for d, x_ in enumerate(dd):
                    gx, ga, a, cidx = x_['gx'], x_['ga'], x_['a'], x_['cidx']
                    P.stt('dve', gx, gx, 0.5, ga, ALU.mult, ALU.mult)
                    A.free(x_['hga'])
                    hh, hd = abuf(4096, F32)
                    for s in range(segs):
                        sl = slice(s * L, (s + 1) * L)
                        init = vcol("slru", cidx) if S else 0.0
                        if d == 0:
                            P.scan(hd[:, sl], a[:, sl], gx[:, sl], init)
                        else:
                            P.scan(hd[:, sl][:, ::-1], a[:, sl][:, ::-1], gx[:, sl][:, ::-1], init)
                    A.free(x_['hgx']); A.free(x_['ha'])
                    if not S:
                        h3 = hd.rearrange("p (s t) -> p s t", s=4)
                        stv = V(cst_t[:, der["st"] + j * 128: der["st"] + (j + 1) * 128], [vt["st"]]) \
                            .rearrange("p (s d n) -> p s d n", s=4, d=2)
                        P.copy('dve', stv[:, :, d, n], h3[:, :, L - 1] if d == 0 else h3[:, :, 0])
                    hs.append((hh, hd))
                P.tt(EW2, hs[0][1], hs[0][1], hs[1][1], ALU.add)
                hm, mv = abuf(2048, BF16, top=True)
                P.stt('dve', mv, hs[0][1], 0.5, t1, ALU.mult, ALU.mult)
                A.free(hs[0][0]); A.free(hs[1][0]); A.free(sa['ht1'])
                mixed.append((hm, mv))

            for n in range(NK + 1):
                if n < NK:
                    stage_A(n)
                    ada_tick(3)
                if n >= 1:
                    stage_B(n - 1)
            for hh, _ in hb:
                A.free(hh)
            wout = dr["w_rec_out"][j].rearrange("(k p) n -> p k n", p=128)
            proj_residual(lambda m: wout[:, :, m * 128:(m + 1) * 128], NK, [m_[1] for m_ in mixed],
                          lambda m: modcol(l, c, 2, m))
            for m_ in mixed:
                A.free(m_[0])
            if not S:
                ph = ps_alloc(1)
                stsrc = V(cst_t[:, der["st"] + j * 128: der["st"] + (j + 1) * 128], [vt["st"]])
                P.transpose(PSb(ph, 0, n=128), stsrc, ident)
                h, stg = abuf(512, F32)
                P.copy('dve', stg[:, 0:128], PSb(ph, 0, n=128))
                ps_free(ph)
                for s in range(4):
                    P.dma('sp', dr["olru"][s, j].rearrange("d (n p) -> (d n) p", p=128),
                          V(stg.ap[s * 32:(s + 1) * 32, 0:128], stg.tiles))
                A.free(h)

        def mla(l, c, S):
            j = l // 2
            KOFF = 256 if S else 0
            TK = KOFF + T
            NKC = TK // 128
            hb = norm_mod(lambda k: vcol("a1", (l * 2 + c) * 16 + k), lambda k: modcol(l, c, 0, k))
            win = dr["w_mla_in"][j].rearrange("(k p) n -> p k n", p=128)
            PERM = [(0, 16), (16, 0), (32, 48), (48, 32)]
            cq = []
            ckv = []
            hkr = hkrp = None
            for mc in range(10 if S else 9):
                hw, wv = abuf(16 * 128 * 2, BF16)
                w3 = wv.rearrange("p (k c) -> p k c", k=16)
                M = 128 if mc < 8 else 64
                if mc < 8:
                    P.dma('pool', w3, win[:, :, mc * 128:(mc + 1) * 128])
                elif mc == 8:
                    P.dma('pool', w3[:, :, 0:64], win[:, :, 1024:1088])
                else:
                    for (dc, sc_) in PERM:
                        P.dma('pool', w3[:, :, dc:dc + 16], win[:, :, 1024 + sc_:1024 + sc_ + 16])
                ph = ps_alloc(2)
                for k in range(NK):
                    for half in range(2):
                        P.mm(PSb(ph, half, parts=M), w3[:, k, 0:M], hb[k][1][:, half * 512:(half + 1) * 512],
                             start=(k == 0), stop=(k == NK - 1))
                A.free(hw)
                ho, ov = abuf(4096, F32, parts=M, top=True)
                P.copy('act', ov, PSv(ph, parts=M))
                ps_free(ph)
                if mc < 4:
                    cq.append((ho, ov))
                elif mc < 8:
                    ckv.append((ho, ov))
                elif mc == 8:
                    hkr, kr32 = ho, ov
                else:
                    hkrp, krp32 = ho, ov
            for hh, _ in hb:
                A.free(hh)
            hr, rs = rstd_of([x_[1] for x_ in cq], 512)
            qlat = []
            for m in range(4):
                hq, qv = abuf(2048, BF16, top=True)
                P.stt('dve', qv, cq[m][1], vcol("g_mla_q", j * 4 + m), rs, ALU.mult, ALU.mult)
                A.free(cq[m][0])
                qlat.append((hq, qv))
            A.free(hr)
            hr, rs = rstd_of([x_[1] for x_ in ckv], 512)
            hck, ck16 = abuf(4 * TK * 2, BF16, top=True)
            ck3 = ck16[:, 0:4 * TK].rearrange("p (m t) -> p m t", m=4)
            for m in range(4):
                P.stt('dve', ckv[m][1], ckv[m][1], vcol("g_mla_kv", j * 4 + m), rs, ALU.mult, ALU.mult)
                P.copy('act', ck3[:, m, KOFF:TK], ckv[m][1])
            A.free(hr)
            hk16, kr16f = abuf(TK * 2, BF16, parts=64, top=True)
            kr16 = kr16f[:, 0:TK]
            if not S:
                for tb in range(8):
                    s, half = tb // 2, tb % 2
                    ph = ps_alloc(1)
                    for m in range(4):
                        P.transpose(PSb(ph, 0)[:, m * 128:(m + 1) * 128], ckv[m][1][:, tb * 128:(tb + 1) * 128], ident)
                    h, stg = abuf(2048, F32)
                    P.copy('dve', stg, PSb(ph, 0))
                    ps_free(ph)
                    P.dma('sp', dr["ockv"][s, j, half * 128:(half + 1) * 128, :], stg)
                    A.free(h)
                ph = ps_alloc(1)
                for tb in range(8):
                    P.transpose(PSb(ph, 0)[:, tb * 64:(tb + 1) * 64], kr32[:, tb * 128:(tb + 1) * 128],
                                V(ident.ap[0:64, 0:64], ident.tiles))
                h, stg = abuf(2048, F32)
                P.copy('dve', stg, PSb(ph, 0))
                ps_free(ph)
                for tb in range(8):
                    s, half = tb // 2, tb % 2
                    P.dma('sp', dr["okr"][s, j, half * 128:(half + 1) * 128, :], stg[:, tb * 64:(tb + 1) * 64])
                A.free(h)
                P.copy('act', kr16, kr32)
            else:
                h, stg = abuf(4096, F32)
                st3 = stg.rearrange("p (a f) -> p a f", a=2)
                for a in range(2):
                    P.dma('sp', st3[:, a, :], dr["cckv"][j, a * 128:(a + 1) * 128, :])
                for m in range(4):
                    ph = ps_alloc(1)
                    for a in range(2):
                        P.transpose(PSb(ph, 0)[:, a * 128:(a + 1) * 128], st3[:, a, m * 128:(m + 1) * 128], ident)
                    P.copy('dve', ck3[:, m, 0:256], PSb(ph, 0)[:, 0:256])
                    ps_free(ph)
                A.free(h)
                h, stg = abuf(512, F32)
                st3 = stg[:, 0:128].rearrange("p (a f) -> p a f", a=2)
                for a in range(2):
                    P.dma('sp', st3[:, a, :], dr["ckr"][j, a * 128:(a + 1) * 128, :])
                ph = ps_alloc(1)
                for a in range(2):
                    P.transpose(PSb(ph, 0, parts=64)[:, a * 128:(a + 1) * 128], st3[:, a, :], ident)
                P.copy('dve', kr16[:, 0:256], PSb(ph, 0, parts=64)[:, 0:256])
                ps_free(ph)
                A.free(h)
                hcc, cc = abuf(4096, F32, parts=64, top=True)
                hss, ss = abuf(4096, F32, parts=64, top=True)
                P.dma('sp', cc, dr["ropec"])
                P.dma('sp', ss, dr["ropes"])
                P.tt('dve', kr32, kr32, cc, ALU.mult)
                P.tt('dve', krp32, krp32, ss, ALU.mult)
                P.tt('dve', kr16[:, 256:TK], kr32, krp32, ALU.add)
                A.free(hkrp)
            A.free(hkr)
            for m in range(4):
                A.free(ckv[m][0])
            ao = [abuf(2048, BF16, top=True) for _ in range(NK)]
            wuq = dr["w_mla_uq"][j].rearrange("(k p) n -> p k n", p=128)
            wuk = dr["w_mla_uk"][j].rearrange("(k p) n -> p k n", p=128)
            wuv = dr["w_mla_uv"][j].rearrange("(k p) n -> p k n", p=128)
            if S:
                probs = [(0, 512, list(range(NKC))), (512, 512, list(range(NKC)))]
            else:
                probs = [(s * 256, 256, [2 * s, 2 * s + 1]) for s in range(4)]
            units = [(h_, pi, ki) for h_ in range(16) for pi in range(len(probs)) for ki in range(len(probs[pi][2]))]
            hd = {}

            def head_prep(h_):
                hq_, wq = abuf(4 * 256 * 2, BF16)
                wq3 = wq.rearrange("p (k c) -> p k c", k=4)
                P.dma('pool', wq3[:, :, 0:192], wuq[:, :, h_ * 192:(h_ + 1) * 192])
                if S:
                    for (dc, sc_) in PERM:
                        P.dma('pool', wq3[:, :, 192 + dc:192 + dc + 16],
                              wuq[:, :, h_ * 192 + 128 + sc_: h_ * 192 + 128 + sc_ + 16])
                hk_, wk = abuf(4 * 128 * 2 * 2, BF16)
                wk3 = wk.rearrange("p (a k c) -> p a k c", a=2, k=4)
                P.dma('pool', wk3[:, 0], wuk[:, :, h_ * 128:(h_ + 1) * 128])
                P.dma('pool', wk3[:, 1], wuv[:, :, h_ * 128:(h_ + 1) * 128])
                hqn, qn = abuf(2048, BF16)
                ph = ps_alloc(2)
                for k in range(4):
                    for half in range(2):
                        P.mm(PSb(ph, half), wq3[:, k, 0:128], qlat[k][1][:, half * 512:(half + 1) * 512],
                             start=(k == 0), stop=(k == 3))
                P.copy('act', qn, PSv(ph))
                ps_free(ph)
                hqr, qr = abuf(2048, BF16, parts=64)
                ph = ps_alloc(2)
                for k in range(4):
                    for half in range(2):
                        P.mm(PSb(ph, half, parts=64), wq3[:, k, 128:192], qlat[k][1][:, half * 512:(half + 1) * 512],
                             start=(k == 0), stop=(k == 3))
                if S:
                    ht1, t1 = abuf(4096, F32, parts=64)
                    ht2, t2 = abuf(4096, F32, parts=64)
                    P.tt('dve', t1, PSv(ph, parts=64), cc, ALU.mult)
                    ps_free(ph)
                    ph2 = ps_alloc(2)
                    for k in range(4):
                        for half in range(2):
                            P.mm(PSb(ph2, half, parts=64), wq3[:, k, 192:256],
                                 qlat[k][1][:, half * 512:(half + 1) * 512], start=(k == 0), stop=(k == 3))
                    P.tt('dve', t2, PSv(ph2, parts=64), ss, ALU.mult)
                    ps_free(ph2)
                    P.tt('dve', qr, t1, t2, ALU.add)
                    A.free(ht1); A.free(ht2)
                else:
                    P.copy('act', qr, PSv(ph, parts=64))
                    ps_free(ph)
                A.free(hq_)
                hkn, knf = abuf(TK * 2, BF16)
                kn = knf[:, 0:TK]
                for t0 in range(0, TK, 512):
                    n_ = min(512, TK - t0)
                    ph = ps_alloc(1)
                    for k in range(4):
                        P.mm(PSb(ph, 0, n=n_), wk3[:, 0, k, :], ck3[:, k, t0:t0 + n_], start=(k == 0), stop=(k == 3))
                    P.copy('act', kn[:, t0:t0 + n_], PSb(ph, 0, n=n_))
                    ps_free(ph)
                hv_, vf = abuf(NKC * 128 * 2, BF16)
                v3 = vf[:, 0:NKC * 128].rearrange("p (a c) -> p a c", a=NKC)
                for t0 in range(0, NKC, 4):
                    n_ = min(4, NKC - t0)
                    ph = ps_alloc(1)
                    for i in range(n_):
                        tcn = t0 + i
                        for k in range(4):
                            P.mm(PSb(ph, 0)[:, i * 128:(i + 1) * 128], ck3[:, k, tcn * 128:(tcn + 1) * 128],
                                 wk3[:, 1, k, :], start=(k == 0), stop=(k == 3))
                    P.copy('dve', v3[:, t0:t0 + n_, :], PSb(ph, 0, n=n_ * 128).rearrange("p (a c) -> p a c", a=n_))
                    ps_free(ph)
                A.free(hk_)
                hd[h_] = dict(qn=qn, qr=qr, kn=kn, v3=v3, hs=[hqn, hqr, hkn, hv_])

            pend = {}

            def emit_S(i):
                h_, pi, ki = units[i]
                if pi == 0 and ki == 0:
                    head_prep(h_)
                    ada_tick(5)
                q0, nq, kcs = probs[pi]
                kc = kcs[ki]
                x = hd[h_]
                ph = ps_alloc(1)
                sp_ = PSb(ph, 0, n=nq)
                P.mm(sp_, x['kn'][:, kc * 128:(kc + 1) * 128], x['qn'][:, q0:q0 + nq], start=True, stop=False)
                P.mm(sp_, kr16[:, kc * 128:(kc + 1) * 128], x['qr'][:, q0:q0 + nq], start=False, stop=True)
                he, ev = abuf(1024, BF16)
                ev = ev[:, 0:nq]
                P.act(ev, sp_, AF.Exp, scale=ATT_SCALE)
                ps_free(ph)
                pend[i] = (he, ev)

            acc = {}

            def emit_PV(i):
                h_, pi, ki = units[i]
                q0, nq, kcs = probs[pi]
                kc = kcs[ki]
                x = hd[h_]
                he, ev = pend.pop(i)
                if ki == 0:
                    acc[(h_, pi)] = (ps_alloc(1), ps_alloc(1))
                po, pl = acc[(h_, pi)]
                last = (ki == len(kcs) - 1)
                P.mm(PSb(po, 0, n=nq), x['v3'][:, kc, :], ev, start=(ki == 0), stop=last)
                P.mm(PSb(pl, 0, n=nq), ones, ev, start=(ki == 0), stop=last)
                A.free(he)
                if last:
                    hrl, rl = abuf(2048, F32)
                    rl = rl[:, 0:nq]
                    P.op('dve', (lambda e, o=rl.ap, i_=PSb(pl, 0, n=nq).ap: e.reciprocal(o, i_)),
                         [PSb(pl, 0, n=nq)], [rl])
                    P.tt('dve', ao[h_][1][:, q0:q0 + nq], PSb(po, 0, n=nq), rl, ALU.mult)
                    A.free(hrl)
                    ps_free(po); ps_free(pl)
                    del acc[(h_, pi)]
                    if pi == len(probs) - 1:
                        for hh in x['hs']:
                            A.free(hh)
                        del hd[h_]

            LA = cfg.get('la_s', 4) if S else cfg.get('la_p', 6)
            for i in range(len(units) + LA):
                if i < len(units):
                    emit_S(i)
                if i >= LA:
                    emit_PV(i - LA)
            if S:
                A.free(hcc); A.free(hss)
            A.free(hck); A.free(hk16)
            for m in range(4):
                A.free(qlat[m][0])
            wo = dr["w_mla_o"][j].rearrange("(k p) n -> p k n", p=128)
            proj_residual(lambda m: wo[:, :, m * 128:(m + 1) * 128], NK, [a_[1] for a_ in ao],
                          lambda m: modcol(l, c, 2, m))
            for a_ in ao:
                A.free(a_[0])

        for pname in passes:
            S = (pname == "S")
            c = 1 if S else 0
            load_x(dr["xs"] if S else dr["xp"])
            for l in cfg.get("layers", range(nlayers)):
                ada_flush(l)
                if cfg.get("mixer", True):
                    if l % 2 == 0:
                        mla(l, c, S)
                    else:
                        rec(l, c, S)
                if cfg.get("ffn", True):
                    ffn(l, c, 1 if S else 4)
            final_norm(dr["ys"] if S else dr["yp"])
        assert not any(A.used), "arena leak"
        assert not any(ps_used), "psum leak"
        P.emit_all(block, esem, dsem)
    return nc, P


def _rope_tables():
    rows = T // 64
    row = np.repeat(np.arange(rows), 64).astype(np.float32)
    col = np.tile(np.arange(64), rows).astype(np.float32)
    inv = (1.0 / (np.float32(10000.0) ** (np.arange(0, 32, 2, dtype=np.float32) / np.float32(32)))).astype(np.float32)
    ar = (row[:, None] * inv).astype(np.float32)
    ac = (col[:, None] * inv).astype(np.float32)
    cr, sr, cc, sc = np.cos(ar).T, np.sin(ar).T, np.cos(ac).T, np.sin(ac).T
    C = np.concatenate([cr, cr, cc, cc], axis=0).astype(np.float32)
    S_ = np.concatenate([-sr, sr, -sc, sc], axis=0).astype(np.float32)
    return np.ascontiguousarray(C), np.ascontiguousarray(S_)


_CACHE = {}


def make_in_maps(inputs, cores):
    C, S_ = _rope_tables()
    ident = np.eye(128, dtype=np.float32)
    f = lambda a: np.ascontiguousarray(np.asarray(a, dtype=np.float32))
    shared = {k_: f(inputs[k_]) for k_ in W_SHAPES}
    maps = []
    for i in cores:
        m = dict(shared)
        m["xp"] = f(inputs["x_prompt"][4 * i:4 * i + 4]).reshape(1024, 2048)
        m["xs"] = f(inputs["x_sample"][i])
        m["cckv"] = f(inputs["cache_ckv"][i])
        m["ckr"] = f(inputs["cache_krope"][i])
        m["slru"] = f(inputs["state_lru"][i])
        m["cvec"] = np.stack([f(inputs["c_ctx"]), f(inputs["c"][i])], axis=0)
        m["ident"] = ident
        m["ropec"] = C
        m["ropes"] = S_
        maps.append(m)
    return maps


def kernel(**inputs):
    n = 8
    if "nc" not in _CACHE:
        _CACHE["nc"] = build_program()[0]
    nc = _CACHE["nc"]
    maps = make_in_maps(inputs, list(range(n)))
    res = run_bass_kernel_spmd(nc, maps, core_ids=list(range(n)))
    r = res.results
    y_prompt = np.concatenate([r[i]["yp"].reshape(4, 256, 2048) for i in range(n)], axis=0)
    y_sample = np.stack([r[i]["ys"] for i in range(n)], axis=0)
    ockv = np.concatenate([r[i]["ockv"] for i in range(n)], axis=0)
    okr = np.concatenate([r[i]["okr"] for i in range(n)], axis=0)
    olru = np.concatenate([r[i]["olru"] for i in range(n)], axis=0)
    return (y_prompt.astype(np.float32), y_sample.astype(np.float32), ockv.astype(np.float32),
            okr.astype(np.float32), olru.astype(np.float32))
```
